# Optimizing a Trainium2 kernel written in Bass

```python
import jax
import jax.numpy as jnp
from jax import lax
import numpy as np

D_MODEL = 1024
BATCH = 32
SEQ = 2048
DEPTH = 2

GRID_W = 64
CTX_LEN = 256
NORM_EPS = 1e-6

MLA_HEADS = 8
MLA_Q_RANK = 256
MLA_KV_RANK = 128
MLA_NOPE = 64
MLA_ROPE = 32
MLA_V = 64
MLA_WIDTH = MLA_HEADS * MLA_V
ROPE_BASE = 10000.0
Q_BLOCK = 128

POOL_WINDOWS = (2, 4, 8, 16)
POOL_WIDTH = 512
POOL_GROUP = POOL_WIDTH // len(POOL_WINDOWS)

GLA_HEADS = 4
GLA_DK = 64
GLA_DV = 128
GLA_KW = GLA_HEADS * GLA_DK
GLA_WIDTH = GLA_HEADS * GLA_DV
GLA_GATE_RANK = 16
GLA_TAU = 16.0
GLA_CHUNK = 64

N_BRANCH = 3

IN_NAMES = ('mla_q', 'mla_kv', 'mla_kr', 'mla_gate', 'pool_x', 'pool_gate',
            'gla_q', 'gla_k', 'gla_v', 'gla_af', 'gla_ab', 'gla_gate', 'merge')
IN_SIZES = (MLA_Q_RANK, MLA_KV_RANK, MLA_ROPE, MLA_WIDTH, POOL_WIDTH, POOL_WIDTH,
            GLA_KW, GLA_KW, GLA_WIDTH, GLA_GATE_RANK, GLA_GATE_RANK, GLA_WIDTH, N_BRANCH * D_MODEL)
D_IN = sum(IN_SIZES)

kernel_name = 'hybrid_mla_pool_gla_prefix_dit'


def rmsnorm(x, g):
    xf = x.astype(jnp.float32)
    y = xf * lax.rsqrt(jnp.mean(xf * xf, axis=-1, keepdims=True) + NORM_EPS)
    return (y * g.astype(jnp.float32)).astype(x.dtype)


def split_columns(z):
    offsets = [int(o) for o in np.cumsum(IN_SIZES)[:-1]]
    return dict(zip(IN_NAMES, jnp.split(z, offsets, axis=-1)))


def flip(t):
    return t[:, ::-1]


def axial_rope_tables(row, col):
    half = MLA_ROPE // 2
    inv = ROPE_BASE ** (-jnp.arange(0, half, 2, dtype=jnp.float32) / half)
    ang_r = row.astype(jnp.float32)[:, None] * inv
    ang_c = col.astype(jnp.float32)[:, None] * inv
    ang = jnp.concatenate([ang_r, ang_r, ang_c, ang_c], axis=-1)
    return jnp.cos(ang), jnp.sin(ang)


def apply_rope(x, cos, sin):
    half = MLA_ROPE // 2
    quarter = half // 2

    def rot(v):
        return jnp.concatenate([-v[..., quarter:], v[..., :quarter]], axis=-1)

    rotated = jnp.concatenate([rot(x[..., :half]), rot(x[..., half:])], axis=-1)
    return (x * cos + rotated * sin).astype(x.dtype)


def softmax_attention(q, k, v, scale):
    s = jnp.einsum('bqhd,bkhd->bhqk', q, k).astype(jnp.float32) * scale
    p = jax.nn.softmax(s, axis=-1).astype(v.dtype)
    return jnp.einsum('bhqk,bkhd->bqhd', p, v)


def blocked_attention(q, k, v, scale):
    B, L, H, dk = q.shape
    nblk = L // Q_BLOCK
    qb = q.reshape(B, nblk, Q_BLOCK, H, dk).transpose(1, 0, 2, 3, 4)
    ob = lax.map(lambda qq: softmax_attention(qq, k, v, scale), qb)
    return ob.transpose(1, 0, 2, 3, 4).reshape(B, L, H, v.shape[-1])


def mla_queries(zz, q_norm, w_uq):
    B, L, _ = zz['mla_q'].shape
    q = (rmsnorm(zz['mla_q'], q_norm) @ w_uq).reshape(B, L, MLA_HEADS, MLA_NOPE + MLA_ROPE)
    return q[..., :MLA_NOPE], q[..., MLA_NOPE:]


def mla_keys_values(zz, kv_norm, w_ukv):
    B, L, _ = zz['mla_kv'].shape
    kv = (rmsnorm(zz['mla_kv'], kv_norm) @ w_ukv).reshape(B, L, MLA_HEADS, MLA_NOPE + MLA_V)
    return kv[..., :MLA_NOPE], kv[..., MLA_NOPE:]


def assemble_keys(k_nope, k_rope):
    B, L, H, _ = k_nope.shape
    return jnp.concatenate([k_nope, jnp.broadcast_to(k_rope[:, :, None, :], (B, L, H, MLA_ROPE))], axis=-1)


def mla_branch(z, zc, cos, sin, q_norm, w_uq, kv_norm, w_ukv, with_ctx_out):
    B, L, _ = z['mla_q'].shape
    scale = (MLA_NOPE + MLA_ROPE) ** -0.5
    q_nope, q_rope = mla_queries(z, q_norm, w_uq)
    q = jnp.concatenate([q_nope, apply_rope(q_rope, cos[:, None, :], sin[:, None, :])], axis=-1)
    k_nope, v = mla_keys_values(z, kv_norm, w_ukv)
    k = assemble_keys(k_nope, apply_rope(z['mla_kr'], cos, sin))
    kc_nope, vc = mla_keys_values(zc, kv_norm, w_ukv)
    kc = assemble_keys(kc_nope, zc['mla_kr'])
    k_all = jnp.concatenate([kc, k], axis=1)
    v_all = jnp.concatenate([vc, v], axis=1)
    y = blocked_attention(q, k_all, v_all, scale).reshape(B, L, MLA_WIDTH)
    y = y * jax.nn.silu(z['mla_gate'])
    if with_ctx_out:
        Bc, Lc, _ = zc['mla_q'].shape
        qc = jnp.concatenate(mla_queries(zc, q_norm, w_uq), axis=-1)
        yc = softmax_attention(qc, kc, vc, scale).reshape(Bc, Lc, MLA_WIDTH) * jax.nn.silu(zc['mla_gate'])
    else:
        yc = None
    return y, yc


def multiscale_pool(u):
    B, L, W = u.shape
    uf = u.astype(jnp.float32)
    csum = jnp.concatenate([jnp.zeros((B, 1, W), jnp.float32), jnp.cumsum(uf, axis=1)], axis=1)
    t = jnp.arange(L)
    outs = []
    for g, w in enumerate(POOL_WINDOWS):
        lo = jnp.clip(t - w // 2, 0, L)
        hi = jnp.clip(t + w // 2, 0, L)
        cs = csum[..., g * POOL_GROUP:(g + 1) * POOL_GROUP]
        count = (hi - lo).astype(jnp.float32)[None, :, None]
        outs.append((jnp.take(cs, hi, axis=1) - jnp.take(cs, lo, axis=1)) / count)
    return (jnp.concatenate(outs, axis=-1) - uf).astype(u.dtype)


def pool_branch(zz, pool_w, pool_scale):
    B, L, _ = zz['pool_x'].shape
    pooled = multiscale_pool(zz['pool_x']).reshape(B, L, len(POOL_WINDOWS), POOL_GROUP)
    mixed = jnp.einsum('blgi,gio->blgo', pooled, pool_w).reshape(B, L, POOL_WIDTH)
    return mixed * pool_scale * jax.nn.silu(zz['pool_gate'])


def gla_scan(q, k, v, log_a, s0, with_out):
    B, L, H, _ = q.shape
    n = L // GLA_CHUNK

    def to_chunks(t):
        return t.reshape(B, n, GLA_CHUNK, H, t.shape[-1]).transpose(1, 0, 3, 2, 4)

    mask = jnp.tril(jnp.ones((GLA_CHUNK, GLA_CHUNK), bool))[:, :, None]

    def step(s, inp):
        qq, kk, vv, aa = inp
        b = jnp.cumsum(aa, axis=2)
        b_last = b[:, :, -1:, :]
        s_new = jnp.exp(b_last)[:, :, 0, :, None] * s + jnp.einsum('bhcd,bhce->bhde', kk * jnp.exp(b_last - b), vv)
        if not with_out:
            return s_new, None
        inter = jnp.einsum('bhcd,bhde->bhce', qq * jnp.exp(b), s)
        decay = jnp.exp(jnp.where(mask, b[:, :, :, None, :] - b[:, :, None, :, :], -jnp.inf))
        attn = jnp.einsum('bhid,bhjd,bhijd->bhij', qq, kk, decay)
        intra = jnp.einsum('bhij,bhje->bhie', attn, vv)
        return s_new, inter + intra

    s_fin, out = lax.scan(step, s0, (to_chunks(q), to_chunks(k), to_chunks(v), to_chunks(log_a)))
    if with_out:
        out = out.transpose(1, 0, 3, 2, 4).reshape(B, L, H, v.shape[-1])
    return s_fin, out


def gla_inputs(zz, af_w2, af_b, ab_w2, ab_b):
    B, L, _ = zz['gla_v'].shape
    f32 = jnp.float32
    q = zz['gla_q'].astype(f32).reshape(B, L, GLA_HEADS, GLA_DK) * GLA_DK ** -0.5
    k = zz['gla_k'].astype(f32).reshape(B, L, GLA_HEADS, GLA_DK)
    v = zz['gla_v'].astype(f32).reshape(B, L, GLA_HEADS, GLA_DV)
    log_a_f = (jax.nn.log_sigmoid((zz['gla_af'] @ af_w2 + af_b).astype(f32)) / GLA_TAU).reshape(B, L, GLA_HEADS, GLA_DK)
    log_a_b = (jax.nn.log_sigmoid((zz['gla_ab'] @ ab_w2 + ab_b).astype(f32)) / GLA_TAU).reshape(B, L, GLA_HEADS, GLA_DK)
    return q, k, v, log_a_f, log_a_b


def gla_output(o, zz, g):
    B, L = o.shape[:2]
    o = rmsnorm(o, g).reshape(B, L, GLA_WIDTH).astype(zz['gla_gate'].dtype)
    return o * jax.nn.silu(zz['gla_gate'])


def gla_branch(z, zc, af_w2, af_b, ab_w2, ab_b, gla_norm, with_ctx_out):
    qc, kc, vc, afc, abc = gla_inputs(zc, af_w2, af_b, ab_w2, ab_b)
    s0 = jnp.zeros((qc.shape[0], GLA_HEADS, GLA_DK, GLA_DV), jnp.float32)
    sc_f, oc_f = gla_scan(qc, kc, vc, afc, s0, with_ctx_out)
    sc_b, oc_b = gla_scan(flip(qc), flip(kc), flip(vc), flip(abc), s0, with_ctx_out)
    q, k, v, af, ab = gla_inputs(z, af_w2, af_b, ab_w2, ab_b)
    _, o_f = gla_scan(q, k, v, af, sc_f, True)
    _, o_b = gla_scan(flip(q), flip(k), flip(v), flip(ab), sc_b, True)
    y = gla_output(o_f + flip(o_b), z, gla_norm)
    yc = gla_output(oc_f + flip(oc_b), zc, gla_norm) if with_ctx_out else None
    return y, yc


def merge_branches(zz, y_mla, y_pool, y_gla, w_bm, w_bp, w_bg, w_out):
    gates = jax.nn.sigmoid(zz['merge'].astype(jnp.float32)).astype(y_mla.dtype)
    g_mla, g_pool, g_gla = jnp.split(gates, N_BRANCH, axis=-1)
    merged = g_mla * (y_mla @ w_bm) + g_pool * (y_pool @ w_bp) + g_gla * (y_gla @ w_bg)
    return merged @ w_out


def trunk_layer(x, xc, mod, mod_c, cos, sin, pre_g, post_g, w_in, mla_q_norm, mla_w_uq, mla_kv_norm,
                mla_w_ukv, pool_w, pool_scale, gla_af_w2, gla_af_b, gla_ab_w2, gla_ab_b, gla_norm,
                w_branch_mla, w_branch_pool, w_branch_gla, w_out, with_ctx_out):
    shift, scale, gate = jnp.split(mod[:, None, :], 3, axis=-1)
    shift_c, scale_c, gate_c = jnp.split(mod_c[None, None, :], 3, axis=-1)
    z = split_columns((rmsnorm(x, pre_g) * (1 + scale) + shift) @ w_in)
    zc = split_columns((rmsnorm(xc, pre_g) * (1 + scale_c) + shift_c) @ w_in)
    y_mla, yc_mla = mla_branch(z, zc, cos, sin, mla_q_norm, mla_w_uq, mla_kv_norm, mla_w_ukv, with_ctx_out)
    y_pool = pool_branch(z, pool_w, pool_scale)
    y_gla, yc_gla = gla_branch(z, zc, gla_af_w2, gla_af_b, gla_ab_w2, gla_ab_b, gla_norm, with_ctx_out)
    out = merge_branches(z, y_mla, y_pool, y_gla, w_branch_mla, w_branch_pool, w_branch_gla, w_out)
    x = x + gate * rmsnorm(out, post_g)
    if with_ctx_out:
        yc_pool = pool_branch(zc, pool_w, pool_scale)
        out_c = merge_branches(zc, yc_mla, yc_pool, yc_gla, w_branch_mla, w_branch_pool, w_branch_gla, w_out)
        xc = xc + gate_c * rmsnorm(out_c, post_g)
    return x, xc


def setup_inputs(seed: int = 0) -> dict:
    key = jax.random.key(seed)
    ks = jax.random.split(key, 24)
    f32 = jnp.float32

    def nrm(k, shape, s):
        return jax.random.normal(k, shape, f32) * s

    def gain(k, n):
        return 1.0 + 0.1 * jax.random.normal(k, (DEPTH, n), f32)

    return {
        'x': nrm(ks[0], (BATCH, SEQ, D_MODEL), 1.0),
        'c': nrm(ks[1], (BATCH, D_MODEL), 1.0),
        'ctx': nrm(ks[2], (BATCH, CTX_LEN, D_MODEL), 1.0),
        'c_ctx': nrm(ks[3], (D_MODEL,), 1.0),
        'mod_w': nrm(ks[4], (DEPTH, D_MODEL, 3 * D_MODEL), 0.5 * D_MODEL ** -0.5),
        'mod_b': nrm(ks[5], (DEPTH, 3 * D_MODEL), 0.02),
        'pre_norm': gain(ks[6], D_MODEL),
        'post_norm': gain(ks[7], D_MODEL),
        'w_in': nrm(ks[8], (DEPTH, D_MODEL, D_IN), D_MODEL ** -0.5),
        'mla_q_norm': gain(ks[9], MLA_Q_RANK),
        'mla_w_uq': nrm(ks[10], (DEPTH, MLA_Q_RANK, MLA_HEADS * (MLA_NOPE + MLA_ROPE)), MLA_Q_RANK ** -0.5),
        'mla_kv_norm': gain(ks[11], MLA_KV_RANK),
        'mla_w_ukv': nrm(ks[12], (DEPTH, MLA_KV_RANK, MLA_HEADS * (MLA_NOPE + MLA_V)), MLA_KV_RANK ** -0.5),
        'pool_w': nrm(ks[13], (DEPTH, len(POOL_WINDOWS), POOL_GROUP, POOL_GROUP), POOL_GROUP ** -0.5),
        'pool_scale': gain(ks[14], POOL_WIDTH),
        'gla_af_w2': nrm(ks[15], (DEPTH, GLA_GATE_RANK, GLA_KW), GLA_GATE_RANK ** -0.5),
        'gla_af_b': nrm(ks[16], (DEPTH, GLA_KW), 0.1),
        'gla_ab_w2': nrm(ks[17], (DEPTH, GLA_GATE_RANK, GLA_KW), GLA_GATE_RANK ** -0.5),
        'gla_ab_b': nrm(ks[18], (DEPTH, GLA_KW), 0.1),
        'gla_norm': gain(ks[19], GLA_DV),
        'w_branch_mla': nrm(ks[20], (DEPTH, MLA_WIDTH, D_MODEL), MLA_WIDTH ** -0.5),
        'w_branch_pool': nrm(ks[21], (DEPTH, POOL_WIDTH, D_MODEL), POOL_WIDTH ** -0.5),
        'w_branch_gla': nrm(ks[22], (DEPTH, GLA_WIDTH, D_MODEL), GLA_WIDTH ** -0.5),
        'w_out': nrm(ks[23], (DEPTH, D_MODEL, D_MODEL), D_MODEL ** -0.5),
    }


def reference(x, c, ctx, c_ctx, mod_w, mod_b, pre_norm, post_norm, w_in, mla_q_norm, mla_w_uq,
              mla_kv_norm, mla_w_ukv, pool_w, pool_scale, gla_af_w2, gla_af_b, gla_ab_w2, gla_ab_b,
              gla_norm, w_branch_mla, w_branch_pool, w_branch_gla, w_out):
    n_tok = x.shape[1]
    rows = n_tok // GRID_W
    row = jnp.repeat(jnp.arange(rows), GRID_W)
    col = jnp.tile(jnp.arange(GRID_W), rows)
    cos, sin = axial_rope_tables(row, col)
    silu_c = jax.nn.silu(c)
    silu_cc = jax.nn.silu(c_ctx)
    xc = ctx
    for l in range(DEPTH):
        mod = silu_c @ mod_w[l] + mod_b[l]
        mod_c = silu_cc @ mod_w[l] + mod_b[l]
        x, xc = trunk_layer(x, xc, mod, mod_c, cos, sin, pre_norm[l], post_norm[l], w_in[l],
                            mla_q_norm[l], mla_w_uq[l], mla_kv_norm[l], mla_w_ukv[l], pool_w[l],
                            pool_scale[l], gla_af_w2[l], gla_af_b[l], gla_ab_w2[l], gla_ab_b[l],
                            gla_norm[l], w_branch_mla[l], w_branch_pool[l], w_branch_gla[l], w_out[l],
                            l < DEPTH - 1)
    return x
```

```python
import numpy as np
import concourse.bass as bass
import concourse.mybir as mybir
from concourse.bass_utils import run_bass_kernel_spmd

F32 = mybir.dt.float32
BF16 = mybir.dt.bfloat16
AF = mybir.ActivationFunctionType
ALU = mybir.AluOpType
ENGS = ("pe", "act", "dve", "pool", "sp")
SEM_LIMIT = 30000
NDMASEM = 6


class Trk:
    __slots__ = ("w", "r")

    def __init__(self):
        self.w = None
        self.r = {}


class V:
    def __init__(self, ap, t):
        self.ap = ap
        self.t = t


class Buf:
    def __init__(self, tensor, nslots=1):
        self.t = tensor
        self.trk = [Trk() for _ in range(nslots)]

    def v(self, idx, slots=None):
        trk = self.trk if slots is None else [self.trk[s] for s in slots]
        return V(self.t[idx], trk)

    def __getitem__(self, idx):
        return V(self.t[idx], self.trk)

    def ap(self, ap, slots=None):
        trk = self.trk if slots is None else [self.trk[s] for s in slots]
        return V(ap, trk)


class Fw:
    def __init__(self, nc):
        self.nc = nc
        self.ops = {e: [] for e in ENGS}
        self.semn = 0
        self.cur = {e: self._newsem(e) for e in ENGS}
        self.cnt = {e: 0 for e in ENGS}
        self.pend = {e: False for e in ENGS}
        self.known = {e: {} for e in ENGS}
        self.owner = {}
        for e in ENGS:
            self.owner[id(self.cur[e])] = e
        self.dsem = {}
        self.dcnt = {}
        self.dnext = {}
        self.allsems = []
        for e in ("sp", "pool", "act"):
            self.dsem[e] = [self._newsem("d" + e) for _ in range(NDMASEM)]
            self.dcnt[e] = [0] * NDMASEM
            self.dnext[e] = 0
        lo, hi = nc.bump_sbuf(196608 - 16640)
        self.sb_lo = lo
        self.sb_hi = hi
        self.sb_off = lo
        self.sb_names = 0
        self.out_marks = []

    def _newsem(self, tag):
        self.semn += 1
        return self.nc.alloc_semaphore(f"s_{tag}_{self.semn}")

    def sb(self, shape, dtype, nslots=1):
        esz = 4 if dtype == F32 else 2
        n = 1
        for s in shape[1:]:
            n *= s
        nbytes = (n * esz + 63) // 64 * 64
        off = self.sb_off
        self.sb_off += nbytes
        assert self.sb_off <= self.sb_hi, f"sbuf overflow {self.sb_off}"
        self.sb_names += 1
        t = self.nc.alloc_sbuf_tensor_at(f"sb{self.sb_names}", list(shape), dtype, offset=off)
        return Buf(t, nslots)

    def mark(self):
        return self.sb_off

    def release(self, m):
        self.sb_off = m

    def _deps(self, reads, writes):
        deps = {}

        def add(m):
            if m is None:
                return
            k = id(m[0])
            if k not in deps or deps[k][1] < m[1]:
                deps[k] = m

        for v in reads:
            for t in v.t:
                add(t.w)
        for v in writes:
            for t in v.t:
                add(t.w)
                for m in t.r.values():
                    add(m)
        return deps

    def _waits(self, eng, deps, same_ok=False):
        waits = []
        kn = self.known[eng]
        for k, (sem, val) in deps.items():
            if same_ok and self.owner.get(k) == eng:
                continue
            if kn.get(k, 0) >= val:
                continue
            kn[k] = val
            waits.append((sem, val))
        return waits

    def _record(self, reads, writes, m):
        for v in reads:
            for t in v.t:
                t.r[id(m[0])] = m
        for v in writes:
            for t in v.t:
                t.w = m
                t.r = {}

    def op(self, eng, fn, reads, writes, inc=True):
        if eng == "pool":
            eng = "dve"
        if inc and not self.pend[eng] and self.cnt[eng] >= SEM_LIMIT:
            self.cur[eng] = self._newsem(eng)
            self.owner[id(self.cur[eng])] = eng
            self.cnt[eng] = 0
        waits = self._waits(eng, self._deps(reads, writes), same_ok=(eng == "pe"))
        sem = self.cur[eng]
        if inc:
            self.cnt[eng] += 1
            m = (sem, self.cnt[eng])
            self.pend[eng] = False
        else:
            m = (sem, self.cnt[eng] + 1)
            self.pend[eng] = True
        self.ops[eng].append((waits, fn, sem if inc else None, 1))
        self._record(reads, writes, m)
        return m

    def dma(self, q, out, in_, **kw):
        i = self.dnext[q]
        self.dnext[q] = (i + 1) % NDMASEM
        sem = self.dsem[q][i]
        deps = self._deps([in_], [out])
        prev = self.dcnt[q][i]
        if prev:
            deps[id(sem)] = (sem, prev)
        waits = self._waits(q, deps)
        self.dcnt[q][i] = prev + 16
        m = (sem, prev + 16)
        oa, ia = out.ap, in_.ap
        self.ops[q].append((waits, lambda e: e.dma_start(out=oa, in_=ia, **kw), sem, 16))
        self._record([in_], [out], m)
        return m

    def barrier(self):
        marks = [(self.cur[e], self.cnt[e]) for e in ENGS if self.cnt[e]]
        for q in self.dsem:
            for s, c in zip(self.dsem[q], self.dcnt[q]):
                if c:
                    marks.append((s, c))
        for e in ENGS:
            deps = {id(m[0]): m for m in marks}
            waits = self._waits(e, deps)
            if waits:
                self.ops[e].append((waits, None, None, 0))

    def emit(self):
        nc = self.nc
        hand = {"pe": "tensor", "act": "scalar", "dve": "vector", "pool": "gpsimd", "sp": "sync"}
        with nc.Block() as block:
            for e in ENGS:
                ops = self.ops[e]

                def body(eng, ops=ops):
                    for waits, fn, sem, n in ops:
                        for s, v in waits:
                            eng.wait_ge(s, v)
                        if fn is None:
                            continue
                        ins = fn(eng)
                        if sem is not None:
                            ins.then_inc(sem, n)

                getattr(block, hand[e])(body)

    def mm(self, out, lhsT, rhs, start=True, stop=True, inc=None):
        if inc is None:
            inc = stop
        o, l, r = out.ap, lhsT.ap, rhs.ap
        return self.op("pe", lambda e: e.matmul(o, l, r, start=start, stop=stop), [lhsT, rhs], [out], inc=inc)

    def tr(self, out, in_, ident, inc=True):
        o, i, d = out.ap, in_.ap, ident.ap
        return self.op("pe", lambda e: e.transpose(o, i, d), [in_, ident], [out], inc=inc)

    def act(self, out, in_, func, bias=None, scale=None, accum=None):
        rd = [in_]
        kw = {}
        if bias is not None:
            if isinstance(bias, V):
                rd.append(bias)
                kw["bias"] = bias.ap
            else:
                kw["bias"] = float(bias)
        if scale is not None:
            if isinstance(scale, V):
                rd.append(scale)
                kw["scale"] = scale.ap
            else:
                kw["scale"] = float(scale)
        wr = [out]
        if accum is not None:
            wr.append(accum)
            kw["accum_out"] = accum.ap
        o, i = out.ap, in_.ap
        return self.op("act", lambda e: e.activation(o, i, func, **kw), rd, wr)

    def tt(self, eng, out, in0, in1, op):
        o, a, b = out.ap, in0.ap, in1.ap
        return self.op(eng, lambda e: e.tensor_tensor(o, a, b, op), [in0, in1], [out])

    def ts(self, eng, out, in0, s1, op0, s2=None, op1=None):
        rd = [in0]
        a1 = s1
        if isinstance(s1, V):
            rd.append(s1)
            a1 = s1.ap
        a2 = s2
        if isinstance(s2, V):
            rd.append(s2)
            a2 = s2.ap
        o, a = out.ap, in0.ap
        if op1 is None:
            return self.op(eng, lambda e: e.tensor_scalar(o, a, a1, None, op0), rd, [out])
        return self.op(eng, lambda e: e.tensor_scalar(o, a, a1, a2, op0, op1), rd, [out])

    def stt(self, out, in0, scalar, in1, op0, op1):
        rd = [in0, in1]
        sc = scalar
        if isinstance(scalar, V):
            rd.append(scalar)
            sc = scalar.ap
        o, a, b = out.ap, in0.ap, in1.ap
        return self.op("dve", lambda e: e.scalar_tensor_tensor(o, a, sc, b, op0, op1), rd, [out])

    def copy(self, eng, out, in_):
        o, i = out.ap, in_.ap
        if eng == "act":
            return self.op("act", lambda e: e.activation(o, i, AF.Copy), [in_], [out])
        return self.op(eng, lambda e: e.tensor_copy(o, i), [in_], [out])

    def memset(self, eng, out, val):
        o = out.ap
        return self.op(eng, lambda e: e.memset(o, val), [], [out])

    def scan(self, out, d0, d1, init, op0, op1):
        o, a, b = out.ap, d0.ap, d1.ap
        return self.op("dve", lambda e: e.tensor_tensor_scan(o, a, b, init, op0, op1), [d0, d1], [out])

    def recip(self, out, in_):
        o, i = out.ap, in_.ap
        return self.op("dve", lambda e: e.reciprocal(o, i), [in_], [out])

    def finish(self, marks):
        deps = {}
        for m in marks:
            k = id(m[0])
            if k not in deps or deps[k][1] < m[1]:
                deps[k] = m
        self.known["sp"] = {}
        waits = self._waits("sp", deps)
        self.ops["sp"].append((waits, None, None, 0))


D = 1024
SEQ = 2048
CTX = 256
T = SEQ + CTX
NT = T // 128
DEPTH = 2
D_IN = 6592
EPS = 1e-6
TC = [(0, 256), (256, 512), (768, 512), (1280, 512), (1792, 512)]
C_Q, C_KV, C_KR, C_MG, C_PX, C_PG, C_GQ, C_GK, C_GV, C_AF, C_AB, C_GG, C_MRG = (
    0, 256, 384, 416, 928, 1440, 1952, 2208, 2464, 2976, 2992, 3008, 3520)
UL = 2336
ATT_SCALE = 96 ** -0.5


def upos(t):
    return t + 8 if t < 256 else t + 24


def chunk_of_tile(t):
    return 0 if t < 2 else 1 + (t - 2) // 4


def sl(a, n):
    return slice(a, a + n)


ALL = slice(None)


def host_consts():
    cm = np.zeros((128, 6, 128), np.float32)
    cm[:, 0] = np.eye(128)
    cm[:, 1] = 1.0
    cm[:, 2, 0:64] = 1.0
    cm[:, 3, 64:128] = 1.0
    j = np.arange(128)[:, None]
    i = np.arange(128)[None, :]
    cm[:, 4] = (j <= i)
    cm[:, 5] = (j >= i)
    half = 16
    inv = 10000.0 ** (-np.arange(0, half, 2, dtype=np.float32) / half)
    tt = np.arange(SEQ)
    row = (tt // 64).astype(np.float32)
    col = (tt % 64).astype(np.float32)
    ang_r = row[:, None] * inv
    ang_c = col[:, None] * inv
    ang = np.concatenate([ang_r, ang_r, ang_c, ang_c], -1).astype(np.float32)
    cs = np.zeros((128, 2, T), np.float32)
    cs[64:96, 0, :CTX] = 1.0
    cs[64:96, 0, CTX:] = np.cos(ang).T
    cs[64:96, 1, CTX:] = np.sin(ang).T
    invc = np.zeros((128, 4, UL), np.float32)
    for g, w in enumerate((2, 4, 8, 16)):
        for (L, base, t0) in ((CTX, 8, 0), (SEQ, 280, 256)):
            t = np.arange(L)
            lo = np.clip(t - w // 2, 0, L)
            hi = np.clip(t + w // 2, 0, L)
            invc[:, g, base:base + L] = (1.0 / (hi - lo).astype(np.float32))[None, :]
    sm = np.ones((128, T), np.float32)
    sm[:, 0::128] = 0.0
    return cm.reshape(128, 768), cs, invc, sm


NV = 40


def host_vecs(inp, l):
    v = np.zeros((128, NV), np.float32)

    def colmaj(a):
        return np.asarray(a, np.float32).reshape(-1, 128).T

    v[:, 0:8] = colmaj(inp["pre_norm"][l])
    v[:, 8:16] = colmaj(inp["mod_b"][l][0:1024])
    v[:, 16:24] = colmaj(inp["mod_b"][l][1024:2048])
    v[:, 24:26] = colmaj(inp["mla_q_norm"][l])
    v[:, 26:27] = colmaj(inp["mla_kv_norm"][l])
    v[:, 27:31] = colmaj(inp["pool_scale"][l])
    v[:, 31:33] = colmaj(inp["gla_af_b"][l])
    v[:, 33:35] = colmaj(inp["gla_ab_b"][l])
    v[:, 35:36] = colmaj(inp["gla_norm"][l])
    return v


class Prog:
    def __init__(self, nseq, layers):
        self.nseq = nseq
        self.layers = layers
        nc = self.nc = bass.Bass("TRN2", target_bir_lowering=False)
        fw = self.fw = Fw(nc)

        def din(name, shape):
            return Buf(nc.dram_tensor(name, list(shape), F32, kind="ExternalInput"))

        self.x_in = din("x", [nseq, SEQ, D])
        self.c_in = din("ctx", [nseq, CTX, D])
        self.w_in = din("w_in", [DEPTH, D, D_IN])
        self.mod_w = din("mod_w", [DEPTH, D, 3 * D])
        self.w_uq = din("mla_w_uq", [DEPTH, 256, 768])
        self.w_ukv = din("mla_w_ukv", [DEPTH, 128, 1024])
        self.pool_w = din("pool_w", [DEPTH, 4, 128, 128])
        self.af_w2 = din("gla_af_w2", [DEPTH, 16, 256])
        self.ab_w2 = din("gla_ab_w2", [DEPTH, 16, 256])
        self.w_b = [din(n, [DEPTH, 512, D]) for n in ("w_branch_mla", "w_branch_pool", "w_branch_gla")]
        self.w_out = din("w_out", [DEPTH, D, D])
        self.vecs = din("vecs", [DEPTH, 128, NV])
        self.rows = din("rows", [DEPTH, 128, 2 * D])
        self.cT = din("cT", [128, 8 * 5])
        self.k_cm = din("k_cm", [128, 768])
        self.k_cs = din("k_cs", [128, 2, T])
        self.k_invc = din("k_invc", [128, 4, UL])
        self.k_sm = din("k_sm", [128, T])
        self.x_out = Buf(nc.dram_tensor("xo", [nseq, SEQ, D], F32, kind="ExternalOutput"), nseq * 16)
        self.c_out = None
        if 0 in layers:
            kind = {"kind": "ExternalOutput"} if layers == [0] else {}
            self.c_out = Buf(nc.dram_tensor("co", [nseq, CTX, D], F32, **kind), nseq * 2)
        self.psb = [Buf(nc.alloc_psum_tensor(f"ps{i}", [128, 512], F32)) for i in range(7)]
        self.pst = Buf(nc.alloc_psum_tensor("pst", [128, 1024], BF16))
        self.psn = 0
        self.out_marks = []
        self.cmb = fw.sb([128, 6, 128], BF16)
        self.scT = fw.sb([128, 8, 5], BF16)
        self.onesf = fw.sb([128, 128], F32)
        mtmp = fw.mark()
        cm = fw.sb([128, 768], F32)
        fw.dma("sp", cm[:], self.k_cm[:, :])
        fw.copy("dve", self.cmb[:], cm.ap(cm.t[:, :].rearrange("p (a b) -> p a b", a=6)))
        self.maskf = self.cmb.v((ALL, 4, ALL))
        self.maskb = self.cmb.v((ALL, 5, ALL))
        self.ident = self.cmb.v((ALL, 0, ALL))
        self.ones = self.cmb.v((ALL, 1, ALL))
        self.onesE = self.cmb.v((ALL, 2, ALL))
        self.onesO = self.cmb.v((ALL, 3, ALL))
        cTs = fw.sb([128, 40], F32)
        fw.dma("sp", cTs[:], self.cT[:, :])
        fw.act(self.scT[:], cTs.ap(cTs.t[:, :].rearrange("p (a b) -> p a b", a=8)), AF.Silu)
        fw.memset("pool", self.onesf[:], 1.0)
        fw.barrier()
        fw.release(mtmp)
        for li, l in enumerate(layers):
            for s in range(nseq):
                self.block(s, l)
        fw.finish(self.out_marks)
        fw.emit()

    def ps(self):
        b = self.psb[self.psn % 5]
        self.psn += 1
        return b

    def wload(self, dst, src_buf, src_ap):
        self.fw.dma("pool", dst, src_buf.ap(src_ap))

    def win_block(self, dst, l, c0, n):
        src = self.w_in.t[l, :, c0:c0 + n].rearrange("(kc p) c -> p kc c", p=128)
        self.wload(dst, self.w_in, src)

    def zT(self, pst, wv, c, hT, m0=0, mn=128):
        a, n = TC[c]
        for kc in range(8):
            self.fw.mm(pst.v((sl(0, mn), sl(0, n))), wv.v((ALL, kc, sl(m0, mn))),
                       hT.v((ALL, kc, sl(a, n)), slots=[c]), start=(kc == 0), stop=(kc == 7))

    def rstd_bcast(self, ss_ps, n, nfeat, lnb, rstd):
        fw = self.fw
        fw.act(lnb.v((ALL, sl(0, n))), ss_ps.v((ALL, sl(0, n))), AF.Ln, scale=1.0 / nfeat, bias=EPS)
        fw.act(rstd.v((ALL, sl(0, n))), lnb.v((ALL, sl(0, n))), AF.Exp, scale=-0.5)

    def block(self, s, l):
        fw = self.fw
        nc = self.nc
        first = (l == self.layers[0])
        last_layer = (l == DEPTH - 1)
        with_ctx = not last_layer
        c_lo = 0 if with_ctx else 1
        t_lo = 0 if with_ctx else 2
        xsrc = self.x_in if first else self.x_out
        csrc = self.c_in if first else self.c_out
        m_blk = fw.mark()

        def xtile_src(t):
            if t < 2:
                return csrc.v((s, sl(t * 128, 128), ALL), slots=None if csrc is self.c_in else [s * 2 + t])
            tt_ = t - 2
            return xsrc.v((s, sl(tt_ * 128, 128), ALL), slots=None if xsrc is self.x_in else [s * 16 + tt_])

        vec = fw.sb([128, NV], F32)
        fw.dma("sp", vec[:], self.vecs.v((l, ALL, ALL)))
        negb = fw.sb([128, 4], F32)
        fw.ts("dve", negb[:], vec.v((ALL, sl(31, 4))), -1.0, ALU.mult)

        Acol = fw.sb([128, 8, 5], F32)
        Scol = fw.sb([128, 8, 5], F32)
        m0 = fw.mark()
        modw = fw.sb([128, 8, 2048], BF16)
        self.wload(modw[:], self.mod_w, self.mod_w.t[l, :, 0:2048].rearrange("(kc p) c -> p kc c", p=128))
        pm = self.ps()
        for ch in range(16):
            for kc in range(8):
                fw.mm(pm.v((ALL, sl(ch * 5, 5))), modw.v((ALL, kc, sl(ch * 128, 128))), self.scT.v((ALL, kc, ALL)),
                      start=(kc == 0), stop=(kc == 7))
        for ch in range(8):
            fw.ts("dve", Scol.v((ALL, ch, ALL)), pm.v((ALL, sl(ch * 5, 5))), vec.v((ALL, sl(8 + ch, 1))), ALU.add)
            fw.ts("dve", Acol.v((ALL, ch, ALL)), pm.v((ALL, sl((8 + ch) * 5, 5))), vec.v((ALL, sl(16 + ch, 1))), ALU.add,
                  1.0, ALU.add)
            fw.ts("dve", Acol.v((ALL, ch, ALL)), Acol.v((ALL, ch, ALL)), vec.v((ALL, sl(ch, 1))), ALU.mult)
        fw.barrier()
        fw.release(m0)

        hT = fw.sb([128, 8, T], BF16, nslots=5)
        ymla = fw.sb([128, 4, T], BF16, nslots=5)
        ypool = fw.sb([128, 4, T], BF16, nslots=5)
        ygla = fw.sb([128, 4, T], BF16, nslots=5)

        m0 = fw.mark()
        xt = [fw.sb([128, D], F32) for _ in range(3)]
        xn = [fw.sb([128, D], BF16) for _ in range(2)]
        junk = fw.sb([128, D], BF16)
        st = [fw.sb([128, 4], F32) for _ in range(2)]
        for t in range(NT):
            c = chunk_of_tile(t)
            j = 4 if t < 2 else s
            X = xt[t % 3]
            fw.dma("sp", X[:], xtile_src(t))
            S_ = st[t % 2]
            fw.act(junk[:], X[:], AF.Square, accum=S_.v((ALL, sl(0, 1))))
            fw.act(S_.v((ALL, sl(1, 1))), S_.v((ALL, sl(0, 1))), AF.Ln, scale=1.0 / D, bias=EPS)
            fw.act(S_.v((ALL, sl(2, 1))), S_.v((ALL, sl(1, 1))), AF.Exp, scale=-0.5)
            XN = xn[t % 2]
            fw.ts("dve", XN[:], X[:], S_.v((ALL, sl(2, 1))), ALU.mult)
            for kc in range(8):
                fw.tr(self.pst.v((ALL, sl(kc * 128, 128))), XN.v((ALL, sl(kc * 128, 128))), self.ident, inc=(kc == 7))
            for kc in range(8):
                o = hT.v((ALL, kc, sl(t * 128, 128)), slots=[c])
                i_ = self.pst.v((ALL, sl(kc * 128, 128)))
                if kc % 2 == 0:
                    fw.act(o, i_, AF.Identity, scale=Acol.v((ALL, kc, sl(j, 1))), bias=Scol.v((ALL, kc, sl(j, 1))))
                else:
                    fw.ts("dve", o, i_, Acol.v((ALL, kc, sl(j, 1))), ALU.mult, Scol.v((ALL, kc, sl(j, 1))), ALU.add)
        fw.barrier()
        fw.release(m0)

        import os as _os
        stop = _os.environ.get("KSTOP", "all")
        if stop == "p0":
            return
        self.mla(s, l, hT, ymla, vec, c_lo)
        if stop == "mla":
            return
        self.pool(s, l, hT, ypool, vec, c_lo)
        if stop == "pool":
            return
        self.gla(s, l, hT, ygla, vec, negb, c_lo, t_lo)
        if stop == "gla":
            return
        self.merge_out(s, l, hT, ymla, ypool, ygla, vec, c_lo, t_lo, xtile_src, last_layer)
        fw.barrier()
        fw.release(m_blk)

    def mla(self, s, l, hT, ymla, vec, c_lo):
        fw = self.fw
        m0 = fw.mark()
        cs = fw.sb([128, 2, 512], F32)

        def load_cs(c):
            a, n = TC[c]
            fw.dma("sp", cs.v((sl(64, 32), ALL, sl(0, n))), self.k_cs.v((sl(64, 32), ALL, sl(a, n))))
        kvn = fw.sb([128, T], BF16, nslots=5)
        krope = fw.sb([128, T], BF16, nslots=5)
        qn = fw.sb([128, 2, T], BF16, nslots=5)
        wukv = fw.sb([128, 1024], BF16)
        self.wload(wukv[:], self.w_ukv, self.w_ukv.t[l, :, :])
        wuq = fw.sb([128, 2, 768], BF16)
        self.wload(wuq[:], self.w_uq, self.w_uq.t[l, :, :].rearrange("(kc p) c -> p kc c", p=128))
        wuqr = fw.sb([128, 2, 768], BF16)
        fw.memset("pool", wuqr[:], 0.0)
        for h in range(8):
            for a in range(2):
                c0 = h * 96 + 64 + 16 * a
                fw.ts("dve", wuqr.v((ALL, ALL, sl(c0, 8))), wuq.v((ALL, ALL, sl(c0 + 8, 8))), -1.0, ALU.mult)
                fw.copy("dve", wuqr.v((ALL, ALL, sl(c0 + 8, 8))), wuq.v((ALL, ALL, sl(c0, 8))))
        sq = [fw.sb([128, 512], BF16) for _ in range(2)]
        lnb = fw.sb([128, 512], F32)
        rstd = fw.sb([128, 512], F32)
        t1 = fw.sb([128, 512], F32)
        t2 = fw.sb([128, 512], F32)

        m1 = fw.mark()
        wkv = fw.sb([128, 8, 160], BF16)
        self.win_block(wkv[:], l, C_KV, 160)
        wkr = fw.sb([128, 8, 96], BF16)
        fw.memset("pool", wkr[:], 0.0)
        for a in range(2):
            c0 = 128 + 16 * a
            d0 = 64 + 16 * a
            fw.ts("dve", wkr.v((ALL, ALL, sl(d0, 8))), wkv.v((ALL, ALL, sl(c0 + 8, 8))), -1.0, ALU.mult)
            fw.copy("dve", wkr.v((ALL, ALL, sl(d0 + 8, 8))), wkv.v((ALL, ALL, sl(c0, 8))))
        for c in range(5):
            a, n = TC[c]
            load_cs(c)
            pkv = self.ps()
            self.zT(pkv, wkv, c, hT, 0, 128)
            pkr = self.ps()
            self.zT(pkr, wkv, c, hT, 64, 96)
            pro = self.ps()
            self.zT(pro, wkr, c, hT, 0, 96)
            SQ = sq[c % 2]
            fw.act(SQ.v((ALL, sl(0, n))), pkv.v((ALL, sl(0, n))), AF.Square)
            pss = self.ps()
            fw.mm(pss.v((ALL, sl(0, n))), self.ones, SQ.v((ALL, sl(0, n))))
            self.rstd_bcast(pss, n, 128, lnb, rstd)
            fw.stt(kvn.v((ALL, sl(a, n)), slots=[c]), pkv.v((ALL, sl(0, n))), vec.v((ALL, sl(26, 1))),
                   rstd.v((ALL, sl(0, n))), ALU.mult, ALU.mult)
            R = sl(64, 32)
            fw.tt("dve", t1.v((R, sl(0, n))), pkr.v((R, sl(0, n))), cs.v((R, 0, sl(0, n))), ALU.mult)
            fw.tt("dve", t2.v((R, sl(0, n))), pro.v((R, sl(0, n))), cs.v((R, 1, sl(0, n))), ALU.mult)
            fw.tt("pool", krope.v((R, sl(a, n)), slots=[c]), t1.v((R, sl(0, n))), t2.v((R, sl(0, n))), ALU.add)
        fw.barrier()
        fw.release(m1)

        m1 = fw.mark()
        wq = fw.sb([128, 8, 256], BF16)
        self.win_block(wq[:], l, C_Q, 256)
        for c in range(c_lo, 5):
            a, n = TC[c]
            pq = [self.ps(), self.ps()]
            pss = self.ps()
            for k2 in range(2):
                self.zT(pq[k2], wq, c, hT, k2 * 128, 128)
                fw.act(sq[k2].v((ALL, sl(0, n))), pq[k2].v((ALL, sl(0, n))), AF.Square)
            for k2 in range(2):
                fw.mm(pss.v((ALL, sl(0, n))), self.ones, sq[k2].v((ALL, sl(0, n))), start=(k2 == 0), stop=(k2 == 1))
            self.rstd_bcast(pss, n, 256, lnb, rstd)
            for k2 in range(2):
                fw.stt(qn.v((ALL, k2, sl(a, n)), slots=[c]), pq[k2].v((ALL, sl(0, n))), vec.v((ALL, sl(24 + k2, 1))),
                       rstd.v((ALL, sl(0, n))), ALU.mult, ALU.mult)
        fw.barrier()
        fw.release(m1)

        KT = [fw.sb([128, T], BF16, nslots=5) for _ in range(2)]
        QT = [fw.sb([128, T], BF16, nslots=5) for _ in range(2)]
        VP = [fw.sb([128, NT, 128], BF16, nslots=NT) for _ in range(2)]
        for i in range(2):
            fw.memset("pool", VP[i][:], 0.0)
        sg = fw.sb([128, T], BF16, nslots=5)
        wg = [fw.sb([128, 8, 128], BF16) for _ in range(2)]
        PT = [fw.sb([128, 512], BF16) for _ in range(4)]
        rec = lnb
        accO, accD = self.psb[5], self.psb[6]
        npt = 0
        for hp in range(4):
            WG = wg[hp % 2]
            self.win_block(WG[:], l, C_MG + hp * 128, 128)
            for c in range(c_lo, 5):
                a, n = TC[c]
                pg = self.ps()
                self.zT(pg, WG, c, hT)
                fw.act(sg.v((ALL, sl(a, n)), slots=[c]), pg.v((ALL, sl(0, n))), AF.Silu)
            for i in range(2):
                h = hp * 2 + i
                for c in range(5):
                    a, n = TC[c]
                    pk = self.ps()
                    fw.mm(pk.v((sl(0, 64), sl(0, n))), wukv.v((ALL, sl(h * 128, 64))), kvn.v((ALL, sl(a, n)), slots=[c]))
                    fw.copy("act", KT[i].v((sl(0, 64), sl(a, n)), slots=[c]), pk.v((sl(0, 64), sl(0, n))))
                    fw.copy("pool", KT[i].v((sl(64, 32), sl(a, n)), slots=[c]), krope.v((sl(64, 32), sl(a, n)), slots=[c]))
                for c in range(c_lo, 5):
                    a, n = TC[c]
                    load_cs(c)
                    p1, p2 = self.ps(), self.ps()
                    for k2 in range(2):
                        fw.mm(p1.v((sl(0, 96), sl(0, n))), wuq.v((ALL, k2, sl(h * 96, 96))),
                              qn.v((ALL, k2, sl(a, n)), slots=[c]), start=(k2 == 0), stop=(k2 == 1))
                    for k2 in range(2):
                        fw.mm(p2.v((sl(0, 96), sl(0, n))), wuqr.v((ALL, k2, sl(h * 96, 96))),
                              qn.v((ALL, k2, sl(a, n)), slots=[c]), start=(k2 == 0), stop=(k2 == 1))
                    fw.copy("act", QT[i].v((sl(0, 64), sl(a, n)), slots=[c]), p1.v((sl(0, 64), sl(0, n))))
                    R = sl(64, 32)
                    fw.tt("dve", t1.v((R, sl(0, n))), p1.v((R, sl(0, n))), cs.v((R, 0, sl(0, n))), ALU.mult)
                    fw.tt("dve", t2.v((R, sl(0, n))), p2.v((R, sl(0, n))), cs.v((R, 1, sl(0, n))), ALU.mult)
                    fw.tt("pool", QT[i].v((R, sl(a, n)), slots=[c]), t1.v((R, sl(0, n))), t2.v((R, sl(0, n))), ALU.add)
            for t in range(NT):
                pv = self.ps()
                vsrc = wukv.ap(wukv.t[:, hp * 256:hp * 256 + 256].rearrange("p (h e) -> p h e", h=2)[:, :, 64:128])
                fw.mm(pv.v((ALL, sl(0, 128))), kvn.v((ALL, sl(t * 128, 128)), slots=[chunk_of_tile(t)]), vsrc)
                fw.copy("act", VP[0].v((ALL, t, sl(0, 64)), slots=[t]), pv.v((ALL, sl(0, 64))))
                fw.copy("dve", VP[1].v((ALL, t, sl(64, 64)), slots=[t]), pv.v((ALL, sl(64, 64))))
            for c in range(c_lo, 5):
                a, n = TC[c]
                kts = range(0, 2) if c == 0 else range(0, NT)
                nk = len(kts)
                for ki, kt in enumerate(kts):
                    pts = []
                    for i in range(2):
                        pS = self.ps()
                        fw.mm(pS.v((ALL, sl(0, n))), KT[i].v((sl(0, 96), sl(kt * 128, 128)), slots=[chunk_of_tile(kt)]),
                              QT[i].v((sl(0, 96), sl(a, n)), slots=[c]))
                        P = PT[npt % 4]
                        npt += 1
                        fw.act(P.v((ALL, sl(0, n))), pS.v((ALL, sl(0, n))), AF.Exp, scale=ATT_SCALE)
                        pts.append(P)
                    fw.mm(accO.v((ALL, sl(0, n))), VP[0].v((ALL, kt, ALL), slots=[kt]), pts[0].v((ALL, sl(0, n))),
                          start=(ki == 0), stop=False, inc=False)
                    fw.mm(accO.v((ALL, sl(0, n))), VP[1].v((ALL, kt, ALL), slots=[kt]), pts[1].v((ALL, sl(0, n))),
                          start=False, stop=(ki == nk - 1), inc=False)
                    fw.mm(accD.v((ALL, sl(0, n))), self.onesE, pts[0].v((ALL, sl(0, n))),
                          start=(ki == 0), stop=False, inc=False)
                    fw.mm(accD.v((ALL, sl(0, n))), self.onesO, pts[1].v((ALL, sl(0, n))),
                          start=False, stop=(ki == nk - 1), inc=True)
                fw.act(rec.v((ALL, sl(0, n))), accD.v((ALL, sl(0, n))), AF.Ln)
                fw.act(rec.v((ALL, sl(0, n))), rec.v((ALL, sl(0, n))), AF.Exp, scale=-1.0)
                fw.tt("dve", t1.v((ALL, sl(0, n))), accO.v((ALL, sl(0, n))), rec.v((ALL, sl(0, n))), ALU.mult)
                fw.tt("pool", ymla.v((ALL, hp, sl(a, n)), slots=[c]), t1.v((ALL, sl(0, n))),
                      sg.v((ALL, sl(a, n)), slots=[c]), ALU.mult)
        fw.barrier()
        fw.release(m0)

    def pool(self, s, l, hT, ypool, vec, c_lo):
        fw = self.fw
        m0 = fw.mark()
        U = fw.sb([128, UL], F32)
        A = fw.sb([128, UL], F32)
        B = fw.sb([128, UL], F32)
        PP = fw.sb([128, UL], BF16)
        invc = fw.sb([128, UL], F32)
        sgp = [fw.sb([128, 512], BF16) for _ in range(2)]
        wu = [fw.sb([128, 8, 128], BF16) for _ in range(2)]
        wg = [fw.sb([128, 8, 128], BF16) for _ in range(2)]
        pw = [fw.sb([128, 128], BF16) for _ in range(2)]
        fw.memset("pool", U[:], 0.0)
        for g, w in enumerate((2, 4, 8, 16)):
            WU, WG, PW = wu[g % 2], wg[g % 2], pw[g % 2]
            self.win_block(WU[:], l, C_PX + g * 128, 128)
            self.win_block(WG[:], l, C_PG + g * 128, 128)
            self.wload(PW[:], self.pool_w, self.pool_w.t[l, g, :, :])
            fw.dma("sp", invc[:], self.k_invc.v((ALL, g, ALL)))
            for c in range(5):
                a, n = TC[c]
                pu = self.ps()
                self.zT(pu, WU, c, hT)
                fw.copy("act", U.v((ALL, sl(upos(a), n))), pu.v((ALL, sl(0, n))))
            src, width = U, 1
            bufs = [A, B]
            bi = 0
            while width < w:
                dst = bufs[bi]
                bi ^= 1
                L = UL - 2 * width + 1
                fw.tt("pool" if width > 1 else "dve", dst.v((ALL, sl(0, L))), src.v((ALL, sl(0, L))),
                      src.v((ALL, sl(width, L))), ALU.add)
                src = dst
                width *= 2
            hw_ = w // 2
            L = UL - 16
            tmp = bufs[bi]
            fw.tt("dve", tmp.v((ALL, sl(8, L))), src.v((ALL, sl(8 - hw_, L))), invc.v((ALL, sl(8, L))), ALU.mult)
            fw.tt("pool", PP.v((ALL, sl(8, L))), tmp.v((ALL, sl(8, L))), U.v((ALL, sl(8, L))), ALU.subtract)
            for c in range(c_lo, 5):
                a, n = TC[c]
                pm = self.ps()
                fw.mm(pm.v((ALL, sl(0, n))), PW[:], PP.v((ALL, sl(upos(a), n))))
                pg = self.ps()
                self.zT(pg, WG, c, hT)
                SG = sgp[c % 2]
                fw.act(SG.v((ALL, sl(0, n))), pg.v((ALL, sl(0, n))), AF.Silu)
                fw.stt(ypool.v((ALL, g, sl(a, n)), slots=[c]), pm.v((ALL, sl(0, n))), vec.v((ALL, sl(27 + g, 1))),
                       SG.v((ALL, sl(0, n))), ALU.mult, ALU.mult)
        fw.barrier()
        fw.release(m0)

    def gla(self, s, l, hT, ygla, vec, negb, c_lo, t_lo):
        fw = self.fw
        m0 = fw.mark()
        obuf = fw.sb([128, 4, T], BF16, nslots=NT)
        m_q = fw.mark()
        elast = fw.sb([128, 2, 2, NT], F32)
        wab = fw.sb([128, 8, 32], BF16)
        self.win_block(wab[:], l, C_AF, 32)
        zab = fw.sb([32, T], BF16, nslots=5)
        wpad = fw.sb([32, 2, 256], BF16)
        fw.memset("pool", wpad[:], 0.0)
        self.wload(wpad.v((sl(0, 16), 0, ALL)), self.af_w2, self.af_w2.t[l, :, :])
        self.wload(wpad.v((sl(16, 16), 1, ALL)), self.ab_w2, self.ab_w2.t[l, :, :])
        for c in range(5):
            a, n = TC[c]
            pz = self.ps()
            self.zT(pz, wab, c, hT, 0, 32)
            fw.copy("act", zab.v((ALL, sl(a, n)), slots=[c]), pz.v((sl(0, 32), sl(0, n))))
        vtm = fw.sb([128, NT, 512], BF16, nslots=NT)
        m1 = fw.mark()
        wv = [fw.sb([128, 8, 256], BF16) for _ in range(2)]
        for i in range(2):
            self.win_block(wv[i][:], l, C_GV + 256 * i, 256)
        for t in range(NT):
            c = chunk_of_tile(t)
            pv = self.ps()
            for i in range(2):
                for kc in range(8):
                    fw.mm(pv.v((ALL, sl(i * 256, 256))), hT.v((ALL, kc, sl(t * 128, 128)), slots=[c]), wv[i].v((ALL, kc, ALL)),
                          start=(kc == 0), stop=(kc == 7))
            fw.copy("dve", vtm.v((ALL, t, ALL), slots=[t]), pv[:])
        fw.barrier()
        fw.release(m1)
        qx = [fw.sb([128, T], BF16, nslots=5) for _ in range(2)]
        kx = [fw.sb([128, T], BF16, nslots=5) for _ in range(2)]
        nat = 0
        for dr in (1, 0):
            m1 = fw.mark()
            sp = fw.sb([128, 512], F32)
            cs = fw.sb([128, 512], F32)
            c2 = fw.sb([128, 512], F32)
            eQ = fw.sb([128, 512], F32)
            eK = fw.sb([128, 512], F32)
            et = fw.sb([128, 512], F32)
            wq = [fw.sb([128, 8, 128], BF16) for _ in range(2)]
            wk = [fw.sb([128, 8, 128], BF16) for _ in range(2)]
            for fc in range(2):
                self.win_block(wq[fc][:], l, C_GQ + fc * 128, 128)
                self.win_block(wk[fc][:], l, C_GK + fc * 128, 128)
            for fc in range(2):
                for c in range(5):
                    a, n = TC[c]
                    nt_ = n // 128
                    px = self.ps()
                    fw.mm(px.v((ALL, sl(0, n))), wpad.v((ALL, dr, sl(fc * 128, 128))), zab.v((ALL, sl(a, n)), slots=[c]))
                    fw.act(et.v((ALL, sl(0, n))), px.v((ALL, sl(0, n))), AF.Exp, scale=-1.0,
                           bias=negb.v((ALL, sl(dr * 2 + fc, 1))))
                    fw.act(sp.v((ALL, sl(0, n))), et.v((ALL, sl(0, n))), AF.Ln, bias=1.0)
                    bufs3 = [sp, cs, c2]
                    cur = 0
                    k = 1
                    while k < 128:
                        nxt = 1 if cur != 1 else 2
                        vi = bufs3[cur].t[:, 0:n].rearrange("p (t i) -> p t i", i=128)
                        vo = bufs3[nxt].t[:, 0:n].rearrange("p (t i) -> p t i", i=128)
                        I, O = bufs3[cur], bufs3[nxt]
                        if dr == 0:
                            fw.tt("dve", O.ap(vo[:, :, k:128]), I.ap(vi[:, :, k:128]), I.ap(vi[:, :, 0:128 - k]), ALU.add)
                            fw.copy("pool", O.ap(vo[:, :, 0:k]), I.ap(vi[:, :, 0:k]))
                        else:
                            fw.tt("dve", O.ap(vo[:, :, 0:128 - k]), I.ap(vi[:, :, 0:128 - k]), I.ap(vi[:, :, k:128]), ALU.add)
                            fw.copy("pool", O.ap(vo[:, :, 128 - k:128]), I.ap(vi[:, :, 128 - k:128]))
                        cur = nxt
                        k *= 2
                    assert cur == 1
                    fw.act(eQ.v((ALL, sl(0, n))), cs.v((ALL, sl(0, n))), AF.Exp, scale=-1.0 / 16)
                    fw.act(eK.v((ALL, sl(0, n))), cs.v((ALL, sl(0, n))), AF.Exp, scale=1.0 / 16)
                    pos = 127 if dr == 0 else 0
                    fw.copy("dve", elast.v((ALL, dr, fc, sl(a // 128, nt_))),
                            eQ.ap(eQ.t[:, 0:n].rearrange("p (t i) -> p t i", i=128)[:, :, pos]))
                    pq = self.ps()
                    self.zT(pq, wq[fc], c, hT)
                    fw.stt(qx[fc].v((ALL, sl(a, n)), slots=[c]), pq.v((ALL, sl(0, n))), 0.125,
                           eQ.v((ALL, sl(0, n))), ALU.mult, ALU.mult)
                    pk = self.ps()
                    self.zT(pk, wk[fc], c, hT)
                    fw.tt("dve", kx[fc].v((ALL, sl(a, n)), slots=[c]), pk.v((ALL, sl(0, n))),
                          eK.v((ALL, sl(0, n))), ALU.mult)
            fw.barrier()
            fw.release(m1)
            m1 = fw.mark()
            ktm = fw.sb([128, NT, 256], BF16, nslots=NT)
            S32 = fw.sb([128, 2, 128], F32, nslots=4)
            Sbf = fw.sb([128, 2, 128], BF16, nslots=4)
            stmp = fw.sb([128, 2, 128], F32, nslots=4)
            atm = [fw.sb([128, 128], BF16) for _ in range(4)]
            osum = [fw.sb([128, 128], F32) for _ in range(2)]
            for t in range(NT):
                c = chunk_of_tile(t)
                for fc in range(2):
                    fw.tr(self.pst.v((ALL, sl(fc * 128, 128))), kx[fc].v((ALL, sl(t * 128, 128)), slots=[c]),
                          self.ident, inc=(fc == 1))
                fw.copy("act", ktm.v((ALL, t, ALL), slots=[t]), self.pst.v((ALL, sl(0, 256))))
            order = ([1, 0] + list(range(NT - 1, 1, -1))) if dr == 1 else list(range(NT))
            mask = self.maskb if dr == 1 else self.maskf
            for oi, t in enumerate(order):
                c = chunk_of_tile(t)
                need_out = t >= t_lo
                for h in range(4):
                    fc, r0 = h // 2, (h % 2) * 64
                    RR = sl(r0, 64)
                    qv = qx[fc].v((RR, sl(t * 128, 128)), slots=[c])
                    kv_ = kx[fc].v((RR, sl(t * 128, 128)), slots=[c])
                    vv = vtm.v((ALL, t, sl(h * 128, 128)), slots=[t])
                    if need_out:
                        pa = self.ps()
                        fw.mm(pa.v((ALL, sl(0, 128))), kv_, qv)
                        AT = atm[nat % 4]
                        nat += 1
                        fw.tt("dve", AT[:], pa.v((ALL, sl(0, 128))), mask, ALU.mult)
                        po = self.ps()
                        fw.mm(po.v((ALL, sl(0, 128))), vv, AT[:], start=True, stop=(oi == 0))
                        if oi > 0:
                            fw.mm(po.v((ALL, sl(0, 128))), Sbf.v((RR, fc, ALL), slots=[h]), qv, start=False, stop=True)
                        ov = obuf.v((ALL, h, sl(t * 128, 128)), slots=[t])
                        if dr == 1:
                            fw.copy("act", ov, po.v((ALL, sl(0, 128))))
                        else:
                            fw.tt("dve", ov, po.v((ALL, sl(0, 128))), ov, ALU.add)
                    if oi == len(order) - 1:
                        continue
                    psu = self.ps()
                    fw.mm(psu.v((ALL, sl(0, 128))), ktm.v((ALL, t, sl(fc * 128, 128)), slots=[t]), vv)
                    ecol = elast.v((RR, dr, fc, sl(t, 1)))
                    sv = S32.v((RR, fc, ALL), slots=[h])
                    if oi == 0:
                        fw.ts("dve", sv, psu.v((RR, sl(0, 128))), ecol, ALU.mult)
                    else:
                        tv = stmp.v((RR, fc, ALL), slots=[h])
                        fw.ts("pool", tv, sv, ecol, ALU.mult)
                        fw.stt(sv, psu.v((RR, sl(0, 128))), ecol, tv, ALU.mult, ALU.add)
                    fw.copy("act", Sbf.v((RR, fc, ALL), slots=[h]), sv)
            fw.barrier()
            fw.release(m1)
        fw.barrier()
        fw.release(m_q)

        sq = [fw.sb([128, 512], BF16) for _ in range(2)]
        lnb = fw.sb([128, 512], F32)
        rstd = fw.sb([128, 512], F32)
        t1 = fw.sb([128, 512], F32)
        sgp = [fw.sb([128, 512], BF16) for _ in range(2)]
        wg = [fw.sb([128, 8, 128], BF16) for _ in range(2)]
        for h in range(4):
            WG = wg[h % 2]
            self.win_block(WG[:], l, C_GG + h * 128, 128)
            for c in range(c_lo, 5):
                a, n = TC[c]
                tl = [t for t in range(NT) if chunk_of_tile(t) == c]
                ov = obuf.v((ALL, h, sl(a, n)), slots=tl)
                SQ = sq[c % 2]
                fw.act(SQ.v((ALL, sl(0, n))), ov, AF.Square)
                pss = self.ps()
                fw.mm(pss.v((ALL, sl(0, n))), self.ones, SQ.v((ALL, sl(0, n))))
                self.rstd_bcast(pss, n, 128, lnb, rstd)
                pg = self.ps()
                self.zT(pg, WG, c, hT)
                SG = sgp[c % 2]
                fw.act(SG.v((ALL, sl(0, n))), pg.v((ALL, sl(0, n))), AF.Silu)
                fw.stt(t1.v((ALL, sl(0, n))), ov, vec.v((ALL, sl(35, 1))), rstd.v((ALL, sl(0, n))), ALU.mult, ALU.mult)
                fw.tt("pool", ygla.v((ALL, h, sl(a, n)), slots=[c]), t1.v((ALL, sl(0, n))), SG.v((ALL, sl(0, n))), ALU.mult)
        fw.barrier()
        fw.release(m0)

    def merge_out(self, s, l, hT, ymla, ypool, ygla, vec, c_lo, t_lo, xtile_src, last_layer):
        fw = self.fw
        m0 = fw.mark()
        merged = fw.sb([128, 8, T], BF16, nslots=5)
        m1 = fw.mark()
        ys = [ymla, ypool, ygla]
        wb = [fw.sb([128, 4, D], BF16) for _ in range(3)]
        for b in range(3):
            self.wload(wb[b][:], self.w_b[b], self.w_b[b].t[l, :, :].rearrange("(kc p) c -> p kc c", p=128))
        wg = [fw.sb([128, 8, 3, 128], BF16) for _ in range(2)]
        gs = [fw.sb([128, 512], BF16) for _ in range(3)]
        ta = fw.sb([128, 512], F32)
        tb = fw.sb([128, 512], F32)
        for d in range(8):
            WG = wg[d % 2]
            for b in range(3):
                src = self.w_in.t[l, :, C_MRG + b * D + d * 128:C_MRG + b * D + d * 128 + 128].rearrange(
                    "(kc p) c -> p kc c", p=128)
                self.wload(WG.v((ALL, ALL, b, ALL)), self.w_in, src)
            for c in range(c_lo, 5):
                a, n = TC[c]
                pj = []
                for b in range(3):
                    pg = self.ps()
                    for kc in range(8):
                        fw.mm(pg.v((ALL, sl(0, n))), WG.v((ALL, kc, b, ALL)), hT.v((ALL, kc, sl(a, n)), slots=[c]),
                              start=(kc == 0), stop=(kc == 7))
                    fw.act(gs[b].v((ALL, sl(0, n))), pg.v((ALL, sl(0, n))), AF.Sigmoid)
                for b in range(3):
                    pp = self.ps() if b < 2 else self.psb[5]
                    for kc in range(4):
                        fw.mm(pp.v((ALL, sl(0, n))), wb[b].v((ALL, kc, sl(d * 128, 128))),
                              ys[b].v((ALL, kc, sl(a, n)), slots=[c]), start=(kc == 0), stop=(kc == 3))
                    pj.append(pp)
                fw.tt("dve", ta.v((ALL, sl(0, n))), pj[0].v((ALL, sl(0, n))), gs[0].v((ALL, sl(0, n))), ALU.mult)
                fw.tt("dve", tb.v((ALL, sl(0, n))), pj[1].v((ALL, sl(0, n))), gs[1].v((ALL, sl(0, n))), ALU.mult)
                fw.tt("pool", ta.v((ALL, sl(0, n))), ta.v((ALL, sl(0, n))), tb.v((ALL, sl(0, n))), ALU.add)
                fw.tt("dve", tb.v((ALL, sl(0, n))), pj[2].v((ALL, sl(0, n))), gs[2].v((ALL, sl(0, n))), ALU.mult)
                fw.tt("pool", merged.v((ALL, d, sl(a, n)), slots=[c]), ta.v((ALL, sl(0, n))), tb.v((ALL, sl(0, n))), ALU.add)
        fw.barrier()
        fw.release(m1)

        GP = {}
        jl = [4, s] if t_lo == 0 else [s]
        for j in jl:
            GP[j] = fw.sb([128, D], F32)
        m2 = fw.mark()
        rows = fw.sb([128, 2 * D], F32)
        fw.dma("sp", rows[:], self.rows.v((l, ALL, ALL)))
        modg = fw.sb([128, 8, D], BF16)
        self.wload(modg[:], self.mod_w, self.mod_w.t[l, :, 2048:3072].rearrange("(kc p) c -> p kc c", p=128))
        rep = fw.sb([128, 8, 128], BF16)
        for j in jl:
            G = GP[j]
            for kc in range(8):
                fw.ts("dve", rep.v((ALL, kc, ALL)), self.onesf[:], self.scT.v((ALL, kc, sl(j, 1))), ALU.mult)
            for hf in range(2):
                pg = self.ps()
                for kc in range(8):
                    fw.mm(pg[:], rep.v((ALL, kc, ALL)), modg.v((ALL, kc, sl(hf * 512, 512))), start=(kc == 0), stop=(kc == 7))
                fw.tt("dve", G.v((ALL, sl(hf * 512, 512))), pg[:], rows.v((ALL, sl(D + hf * 512, 512))), ALU.add)
                fw.tt("pool", G.v((ALL, sl(hf * 512, 512))), G.v((ALL, sl(hf * 512, 512))), rows.v((ALL, sl(hf * 512, 512))), ALU.mult)
        fw.barrier()
        fw.release(m2)
        wout = fw.sb([128, 8, D], BF16)
        self.wload(wout[:], self.w_out, self.w_out.t[l, :, :].rearrange("(kc p) c -> p kc c", p=128))
        xt = [fw.sb([128, D], F32) for _ in range(2)]
        tm = [fw.sb([128, D], F32) for _ in range(2)]
        xo = tm
        junk = fw.sb([128, 512], BF16)
        st = [fw.sb([128, 8], F32) for _ in range(2)]
        for t in range(t_lo, NT):
            c = chunk_of_tile(t)
            j = 4 if t < 2 else s
            X = xt[t % 2]
            fw.dma("sp", X[:], xtile_src(t))
            po = [self.ps(), self.ps()]
            S_ = st[t % 2]
            for hf in range(2):
                for kc in range(8):
                    fw.mm(po[hf][:], merged.v((ALL, kc, sl(t * 128, 128)), slots=[c]), wout.v((ALL, kc, sl(hf * 512, 512))),
                          start=(kc == 0), stop=(kc == 7))
                fw.act(junk[:], po[hf][:], AF.Square, accum=S_.v((ALL, sl(hf, 1))))
            fw.tt("dve", S_.v((ALL, sl(2, 1))), S_.v((ALL, sl(0, 1))), S_.v((ALL, sl(1, 1))), ALU.add)
            fw.act(S_.v((ALL, sl(3, 1))), S_.v((ALL, sl(2, 1))), AF.Ln, scale=1.0 / D, bias=EPS)
            fw.act(S_.v((ALL, sl(4, 1))), S_.v((ALL, sl(3, 1))), AF.Exp, scale=-0.5)
            TM, XO = tm[t % 2], xo[t % 2]
            for hf in range(2):
                H = sl(hf * 512, 512)
                fw.stt(TM.v((ALL, H)), po[hf][:], S_.v((ALL, sl(4, 1))), GP[j].v((ALL, H)), ALU.mult, ALU.mult)
            fw.tt("pool", XO[:], TM[:], X[:], ALU.add)
            if t < 2:
                dst = self.c_out.v((s, sl(t * 128, 128), ALL), slots=[s * 2 + t])
            else:
                dst = self.x_out.v((s, sl((t - 2) * 128, 128), ALL), slots=[s * 16 + t - 2])
            mk = fw.dma("sp", dst, XO[:])
            if last_layer or (t < 2 and self.layers == [0]) or self.layers == [0]:
                self.out_marks.append(mk)
        fw.barrier()
        fw.release(m0)


_CACHE = {}


def _prog(nseq, layers):
    key = (nseq, tuple(layers))
    if key not in _CACHE:
        _CACHE[key] = Prog(nseq, list(layers)).nc
    return _CACHE[key]


FUSED = False
LAYER_FUSED = True


def kernel(**inp):
    inp = {k: np.asarray(v) for k, v in inp.items()}
    ncore, nseq = 8, 4
    cm, cs, invc, sm = host_consts()
    x = np.ascontiguousarray(inp["x"], np.float32)
    ctx = np.ascontiguousarray(inp["ctx"], np.float32)
    shared = {k: np.ascontiguousarray(inp[k], np.float32) for k in (
        "w_in", "mod_w", "mla_w_uq", "mla_w_ukv", "pool_w", "gla_af_w2", "gla_ab_w2",
        "w_branch_mla", "w_branch_pool", "w_branch_gla", "w_out")}
    shared["vecs"] = np.stack([host_vecs(inp, l) for l in range(DEPTH)])
    shared["rows"] = np.stack([np.concatenate([
        np.broadcast_to(inp["post_norm"][l][None, :], (128, D)),
        np.broadcast_to(inp["mod_b"][l][None, 2048:3072], (128, D))], 1) for l in range(DEPTH)]).astype(np.float32)
    shared.update(k_cm=cm, k_cs=cs, k_invc=invc, k_sm=sm)
    def cT_of(rows5):
        return np.ascontiguousarray(rows5.T.reshape(8, 128, 5).transpose(1, 0, 2).reshape(128, 40), np.float32)

    xs = [np.ascontiguousarray(x[k * nseq:(k + 1) * nseq]) for k in range(ncore)]
    cx = [np.ascontiguousarray(ctx[k * nseq:(k + 1) * nseq]) for k in range(ncore)]
    zero3 = np.zeros((3, D), np.float32)
    if FUSED:
        nc = _prog(nseq, [0, 1])
        in_maps = []
        for k in range(ncore):
            cc = np.concatenate([inp["c"][k * nseq:(k + 1) * nseq], inp["c_ctx"][None, :]], 0)
            in_maps.append(dict(shared, x=xs[k], ctx=cx[k], cT=cT_of(cc)))
        res = run_bass_kernel_spmd(nc, in_maps, core_ids=list(range(ncore)))
        xs = [np.asarray(res.results[k]["xo"]) for k in range(ncore)]
        return np.concatenate(xs, 0).astype(np.float32)
    if LAYER_FUSED:
        nc = _prog(1, [0, 1])
        outs = [np.empty_like(a) for a in xs]
        for j in range(nseq):
            in_maps = []
            for k in range(ncore):
                cc = np.concatenate([inp["c"][k * nseq + j][None, :], zero3, inp["c_ctx"][None, :]], 0)
                in_maps.append(dict(shared, x=xs[k][j:j + 1], ctx=cx[k][j:j + 1], cT=cT_of(cc)))
            res = run_bass_kernel_spmd(nc, in_maps, core_ids=list(range(ncore)))
            for k in range(ncore):
                outs[k][j] = np.asarray(res.results[k]["xo"])[0]
        return np.concatenate(outs, 0).astype(np.float32)
    for l in range(DEPTH):
        nc = _prog(1, [l])
        nxs = [np.empty_like(a) for a in xs]
        ncx = [np.empty_like(a) for a in cx]
        for j in range(nseq):
            in_maps = []
            for k in range(ncore):
                cc = np.concatenate([inp["c"][k * nseq + j][None, :], zero3, inp["c_ctx"][None, :]], 0)
                in_maps.append(dict(shared, x=xs[k][j:j + 1], ctx=cx[k][j:j + 1], cT=cT_of(cc)))
            res = run_bass_kernel_spmd(nc, in_maps, core_ids=list(range(ncore)))
            for k in range(ncore):
                nxs[k][j] = np.asarray(res.results[k]["xo"])[0]
                if l == 0:
                    ncx[k][j] = np.asarray(res.results[k]["co"])[0]
        xs = nxs
        if l == 0:
            cx = ncx
    return np.concatenate(xs, 0).astype(np.float32)
```

```python
import numpy as np
import concourse.bass as bass
import concourse.mybir as mybir
from concourse.bass_utils import run_bass_kernel_spmd

F32 = mybir.dt.float32
BF16 = mybir.dt.bfloat16
AF = mybir.ActivationFunctionType
ALU = mybir.AluOpType
ENGS = ("pe", "act", "dve", "pool", "sp")
SEM_LIMIT = 30000
NDMASEM = 6


class Trk:
    __slots__ = ("w", "r")

    def __init__(self):
        self.w = None
        self.r = {}


class V:
    def __init__(self, ap, t):
        self.ap = ap
        self.t = t


class Buf:
    def __init__(self, tensor, nslots=1):
        self.t = tensor
        self.trk = [Trk() for _ in range(nslots)]

    def v(self, idx, slots=None):
        trk = self.trk if slots is None else [self.trk[s] for s in slots]
        return V(self.t[idx], trk)

    def __getitem__(self, idx):
        return V(self.t[idx], self.trk)

    def ap(self, ap, slots=None):
        trk = self.trk if slots is None else [self.trk[s] for s in slots]
        return V(ap, trk)


class Fw:
    def __init__(self, nc):
        self.nc = nc
        self.ops = {e: [] for e in ENGS}
        self.semn = 0
        self.cur = {e: self._newsem(e) for e in ENGS}
        self.cnt = {e: 0 for e in ENGS}
        self.pend = {e: False for e in ENGS}
        self.known = {e: {} for e in ENGS}
        self.owner = {}
        for e in ENGS:
            self.owner[id(self.cur[e])] = e
        self.dsem = {}
        self.dcnt = {}
        self.dnext = {}
        self.allsems = []
        for e in ("sp", "pool", "act"):
            self.dsem[e] = [self._newsem("d" + e) for _ in range(NDMASEM)]
            self.dcnt[e] = [0] * NDMASEM
            self.dnext[e] = 0
        lo, hi = nc.bump_sbuf(196608 - 16640)
        self.sb_lo = lo
        self.sb_hi = hi
        self.sb_off = lo
        self.sb_names = 0
        self.out_marks = []

    def _newsem(self, tag):
        self.semn += 1
        return self.nc.alloc_semaphore(f"s_{tag}_{self.semn}")

    def sb(self, shape, dtype, nslots=1):
        esz = 4 if dtype == F32 else 2
        n = 1
        for s in shape[1:]:
            n *= s
        nbytes = (n * esz + 63) // 64 * 64
        off = self.sb_off
        self.sb_off += nbytes
        assert self.sb_off <= self.sb_hi, f"sbuf overflow {self.sb_off}"
        self.sb_names += 1
        t = self.nc.alloc_sbuf_tensor_at(f"sb{self.sb_names}", list(shape), dtype, offset=off)
        return Buf(t, nslots)

    def mark(self):
        return self.sb_off

    def release(self, m):
        self.sb_off = m

    def _deps(self, reads, writes):
        deps = {}

        def add(m):
            if m is None:
                return
            k = id(m[0])
            if k not in deps or deps[k][1] < m[1]:
                deps[k] = m

        for v in reads:
            for t in v.t:
                add(t.w)
        for v in writes:
            for t in v.t:
                add(t.w)
                for m in t.r.values():
                    add(m)
        return deps

    def _waits(self, eng, deps, same_ok=False):
        waits = []
        kn = self.known[eng]
        for k, (sem, val) in deps.items():
            if same_ok and self.owner.get(k) == eng:
                continue
            if kn.get(k, 0) >= val:
                continue
            kn[k] = val
            waits.append((sem, val))
        return waits

    def _record(self, reads, writes, m):
        for v in reads:
            for t in v.t:
                t.r[id(m[0])] = m
        for v in writes:
            for t in v.t:
                t.w = m
                t.r = {}

    def op(self, eng, fn, reads, writes, inc=True):
        if eng == "pool":
            eng = "dve"
        if inc and not self.pend[eng] and self.cnt[eng] >= SEM_LIMIT:
            self.cur[eng] = self._newsem(eng)
            self.owner[id(self.cur[eng])] = eng
            self.cnt[eng] = 0
        waits = self._waits(eng, self._deps(reads, writes), same_ok=(eng == "pe"))
        sem = self.cur[eng]
        if inc:
            self.cnt[eng] += 1
            m = (sem, self.cnt[eng])
            self.pend[eng] = False
        else:
            m = (sem, self.cnt[eng] + 1)
            self.pend[eng] = True
        self.ops[eng].append((waits, fn, sem if inc else None, 1))
        self._record(reads, writes, m)
        return m

    def dma(self, q, out, in_, **kw):
        i = self.dnext[q]
        self.dnext[q] = (i + 1) % NDMASEM
        sem = self.dsem[q][i]
        deps = self._deps([in_], [out])
        prev = self.dcnt[q][i]
        if prev:
            deps[id(sem)] = (sem, prev)
        waits = self._waits(q, deps)
        self.dcnt[q][i] = prev + 16
        m = (sem, prev + 16)
        oa, ia = out.ap, in_.ap
        self.ops[q].append((waits, lambda e: e.dma_start(out=oa, in_=ia, **kw), sem, 16))
        self._record([in_], [out], m)
        return m

    def barrier(self):
        marks = [(self.cur[e], self.cnt[e]) for e in ENGS if self.cnt[e]]
        for q in self.dsem:
            for s, c in zip(self.dsem[q], self.dcnt[q]):
                if c:
                    marks.append((s, c))
        for e in ENGS:
            deps = {id(m[0]): m for m in marks}
            waits = self._waits(e, deps)
            if waits:
                self.ops[e].append((waits, None, None, 0))

    def emit(self):
        nc = self.nc
        hand = {"pe": "tensor", "act": "scalar", "dve": "vector", "pool": "gpsimd", "sp": "sync"}
        with nc.Block() as block:
            for e in ENGS:
                ops = self.ops[e]

                def body(eng, ops=ops):
                    for waits, fn, sem, n in ops:
                        for s, v in waits:
                            eng.wait_ge(s, v)
                        if fn is None:
                            continue
                        ins = fn(eng)
                        if sem is not None:
                            ins.then_inc(sem, n)

                getattr(block, hand[e])(body)

    def mm(self, out, lhsT, rhs, start=True, stop=True, inc=None):
        if inc is None:
            inc = stop
        o, l, r = out.ap, lhsT.ap, rhs.ap
        return self.op("pe", lambda e: e.matmul(o, l, r, start=start, stop=stop), [lhsT, rhs], [out], inc=inc)

    def tr(self, out, in_, ident, inc=True):
        o, i, d = out.ap, in_.ap, ident.ap
        return self.op("pe", lambda e: e.transpose(o, i, d), [in_, ident], [out], inc=inc)

    def act(self, out, in_, func, bias=None, scale=None, accum=None):
        rd = [in_]
        kw = {}
        if bias is not None:
            if isinstance(bias, V):
                rd.append(bias)
                kw["bias"] = bias.ap
            else:
                kw["bias"] = float(bias)
        if scale is not None:
            if isinstance(scale, V):
                rd.append(scale)
                kw["scale"] = scale.ap
            else:
                kw["scale"] = float(scale)
        wr = [out]
        if accum is not None:
            wr.append(accum)
            kw["accum_out"] = accum.ap
        o, i = out.ap, in_.ap
        return self.op("act", lambda e: e.activation(o, i, func, **kw), rd, wr)

    def tt(self, eng, out, in0, in1, op):
        o, a, b = out.ap, in0.ap, in1.ap
        return self.op(eng, lambda e: e.tensor_tensor(o, a, b, op), [in0, in1], [out])

    def ts(self, eng, out, in0, s1, op0, s2=None, op1=None):
        rd = [in0]
        a1 = s1
        if isinstance(s1, V):
            rd.append(s1)
            a1 = s1.ap
        a2 = s2
        if isinstance(s2, V):
            rd.append(s2)
            a2 = s2.ap
        o, a = out.ap, in0.ap
        if op1 is None:
            return self.op(eng, lambda e: e.tensor_scalar(o, a, a1, None, op0), rd, [out])
        return self.op(eng, lambda e: e.tensor_scalar(o, a, a1, a2, op0, op1), rd, [out])

    def stt(self, out, in0, scalar, in1, op0, op1):
        rd = [in0, in1]
        sc = scalar
        if isinstance(scalar, V):
            rd.append(scalar)
            sc = scalar.ap
        o, a, b = out.ap, in0.ap, in1.ap
        return self.op("dve", lambda e: e.scalar_tensor_tensor(o, a, sc, b, op0, op1), rd, [out])

    def copy(self, eng, out, in_):
        o, i = out.ap, in_.ap
        if eng == "act":
            return self.op("act", lambda e: e.activation(o, i, AF.Copy), [in_], [out])
        return self.op(eng, lambda e: e.tensor_copy(o, i), [in_], [out])

    def memset(self, eng, out, val):
        o = out.ap
        return self.op(eng, lambda e: e.memset(o, val), [], [out])

    def scan(self, out, d0, d1, init, op0, op1):
        o, a, b = out.ap, d0.ap, d1.ap
        return self.op("dve", lambda e: e.tensor_tensor_scan(o, a, b, init, op0, op1), [d0, d1], [out])

    def recip(self, out, in_):
        o, i = out.ap, in_.ap
        return self.op("dve", lambda e: e.reciprocal(o, i), [in_], [out])

    def finish(self, marks):
        deps = {}
        for m in marks:
            k = id(m[0])
            if k not in deps or deps[k][1] < m[1]:
                deps[k] = m
        self.known["sp"] = {}
        waits = self._waits("sp", deps)
        self.ops["sp"].append((waits, None, None, 0))


D = 1024
SEQ = 2048
CTX = 256
T = SEQ + CTX
NT = T // 128
DEPTH = 2
D_IN = 6592
EPS = 1e-6
TC = [(0, 256), (256, 512), (768, 512), (1280, 512), (1792, 512)]
C_Q, C_KV, C_KR, C_MG, C_PX, C_PG, C_GQ, C_GK, C_GV, C_AF, C_AB, C_GG, C_MRG = (
    0, 256, 384, 416, 928, 1440, 1952, 2208, 2464, 2976, 2992, 3008, 3520)
UL = 2336
ATT_SCALE = 96 ** -0.5


def upos(t):
    return t + 8 if t < 256 else t + 24


def chunk_of_tile(t):
    return 0 if t < 2 else 1 + (t - 2) // 4


def sl(a, n):
    return slice(a, a + n)


ALL = slice(None)


def host_consts():
    cm = np.zeros((128, 6, 128), np.float32)
    cm[:, 0] = np.eye(128)
    cm[:, 1] = 1.0
    cm[:, 2, 0:64] = 1.0
    cm[:, 3, 64:128] = 1.0
    j = np.arange(128)[:, None]
    i = np.arange(128)[None, :]
    cm[:, 4] = (j <= i)
    cm[:, 5] = (j >= i)
    half = 16
    inv = 10000.0 ** (-np.arange(0, half, 2, dtype=np.float32) / half)
    tt = np.arange(SEQ)
    row = (tt // 64).astype(np.float32)
    col = (tt % 64).astype(np.float32)
    ang_r = row[:, None] * inv
    ang_c = col[:, None] * inv
    ang = np.concatenate([ang_r, ang_r, ang_c, ang_c], -1).astype(np.float32)
    cs = np.zeros((128, 2, T), np.float32)
    cs[64:96, 0, :CTX] = 1.0
    cs[64:96, 0, CTX:] = np.cos(ang).T
    cs[64:96, 1, CTX:] = np.sin(ang).T
    invc = np.zeros((128, 4, UL), np.float32)
    for g, w in enumerate((2, 4, 8, 16)):
        for (L, base, t0) in ((CTX, 8, 0), (SEQ, 280, 256)):
            t = np.arange(L)
            lo = np.clip(t - w // 2, 0, L)
            hi = np.clip(t + w // 2, 0, L)
            invc[:, g, base:base + L] = (1.0 / (hi - lo).astype(np.float32))[None, :]
    sm = np.ones((128, T), np.float32)
    sm[:, 0::128] = 0.0
    return cm.reshape(128, 768), cs, invc, sm


NV = 40


def host_vecs(inp, l):
    v = np.zeros((128, NV), np.float32)

    def colmaj(a):
        return np.asarray(a, np.float32).reshape(-1, 128).T

    v[:, 0:8] = colmaj(inp["pre_norm"][l])
    v[:, 8:16] = colmaj(inp["mod_b"][l][0:1024])
    v[:, 16:24] = colmaj(inp["mod_b"][l][1024:2048])
    v[:, 24:26] = colmaj(inp["mla_q_norm"][l])
    v[:, 26:27] = colmaj(inp["mla_kv_norm"][l])
    v[:, 27:31] = colmaj(inp["pool_scale"][l])
    v[:, 31:33] = colmaj(inp["gla_af_b"][l])
    v[:, 33:35] = colmaj(inp["gla_ab_b"][l])
    v[:, 35:36] = colmaj(inp["gla_norm"][l])
    return v


class Prog:
    def __init__(self, nseq, layers):
        self.nseq = nseq
        self.layers = layers
        nc = self.nc = bass.Bass("TRN2", target_bir_lowering=False)
        fw = self.fw = Fw(nc)

        def din(name, shape):
            return Buf(nc.dram_tensor(name, list(shape), F32, kind="ExternalInput"))

        self.x_in = din("x", [nseq, SEQ, D])
        self.c_in = din("ctx", [nseq, CTX, D])
        self.w_in = din("w_in", [DEPTH, D, D_IN])
        self.mod_w = din("mod_w", [DEPTH, D, 3 * D])
        self.w_uq = din("mla_w_uq", [DEPTH, 256, 768])
        self.w_ukv = din("mla_w_ukv", [DEPTH, 128, 1024])
        self.pool_w = din("pool_w", [DEPTH, 4, 128, 128])
        self.af_w2 = din("gla_af_w2", [DEPTH, 16, 256])
        self.ab_w2 = din("gla_ab_w2", [DEPTH, 16, 256])
        self.w_b = [din(n, [DEPTH, 512, D]) for n in ("w_branch_mla", "w_branch_pool", "w_branch_gla")]
        self.w_out = din("w_out", [DEPTH, D, D])
        self.vecs = din("vecs", [DEPTH, 128, NV])
        self.rows = din("rows", [DEPTH, 128, 2 * D])
        self.cT = din("cT", [128, 8 * 5])
        self.k_cm = din("k_cm", [128, 768])
        self.k_cs = din("k_cs", [128, 2, T])
        self.k_invc = din("k_invc", [128, 4, UL])
        self.k_sm = din("k_sm", [128, T])
        self.x_out = Buf(nc.dram_tensor("xo", [nseq, SEQ, D], F32, kind="ExternalOutput"), nseq * 16)
        self.c_out = None
        if 0 in layers:
            kind = {"kind": "ExternalOutput"} if layers == [0] else {}
            self.c_out = Buf(nc.dram_tensor("co", [nseq, CTX, D], F32, **kind), nseq * 2)
        self.psb = [Buf(nc.alloc_psum_tensor(f"ps{i}", [128, 512], F32)) for i in range(7)]
        self.pst = Buf(nc.alloc_psum_tensor("pst", [128, 1024], BF16))
        self.psn = 0
        self.out_marks = []
        self.cmb = fw.sb([128, 6, 128], BF16)
        self.scT = fw.sb([128, 8, 5], BF16)
        self.onesf = fw.sb([128, 128], F32)
        mtmp = fw.mark()
        cm = fw.sb([128, 768], F32)
        fw.dma("sp", cm[:], self.k_cm[:, :])
        fw.copy("dve", self.cmb[:], cm.ap(cm.t[:, :].rearrange("p (a b) -> p a b", a=6)))
        self.maskf = self.cmb.v((ALL, 4, ALL))
        self.maskb = self.cmb.v((ALL, 5, ALL))
        self.ident = self.cmb.v((ALL, 0, ALL))
        self.ones = self.cmb.v((ALL, 1, ALL))
        self.onesE = self.cmb.v((ALL, 2, ALL))
        self.onesO = self.cmb.v((ALL, 3, ALL))
        cTs = fw.sb([128, 40], F32)
        fw.dma("sp", cTs[:], self.cT[:, :])
        fw.act(self.scT[:], cTs.ap(cTs.t[:, :].rearrange("p (a b) -> p a b", a=8)), AF.Silu)
        fw.memset("pool", self.onesf[:], 1.0)
        fw.barrier()
        fw.release(mtmp)
        for li, l in enumerate(layers):
            for s in range(nseq):
                self.block(s, l)
        fw.finish(self.out_marks)
        fw.emit()

    def ps(self):
        b = self.psb[self.psn % 5]
        self.psn += 1
        return b

    def wload(self, dst, src_buf, src_ap):
        self.fw.dma("pool", dst, src_buf.ap(src_ap))

    def win_block(self, dst, l, c0, n):
        src = self.w_in.t[l, :, c0:c0 + n].rearrange("(kc p) c -> p kc c", p=128)
        self.wload(dst, self.w_in, src)

    def zT(self, pst, wv, c, hT, m0=0, mn=128):
        a, n = TC[c]
        for kc in range(8):
            self.fw.mm(pst.v((sl(0, mn), sl(0, n))), wv.v((ALL, kc, sl(m0, mn))),
                       hT.v((ALL, kc, sl(a, n)), slots=[c]), start=(kc == 0), stop=(kc == 7))

    def rstd_bcast(self, ss_ps, n, nfeat, lnb, rstd):
        fw = self.fw
        fw.act(lnb.v((ALL, sl(0, n))), ss_ps.v((ALL, sl(0, n))), AF.Ln, scale=1.0 / nfeat, bias=EPS)
        fw.act(rstd.v((ALL, sl(0, n))), lnb.v((ALL, sl(0, n))), AF.Exp, scale=-0.5)

    def block(self, s, l):
        fw = self.fw
        nc = self.nc
        first = (l == self.layers[0])
        last_layer = (l == DEPTH - 1)
        with_ctx = not last_layer
        c_lo = 0 if with_ctx else 1
        t_lo = 0 if with_ctx else 2
        xsrc = self.x_in if first else self.x_out
        csrc = self.c_in if first else self.c_out
        m_blk = fw.mark()

        def xtile_src(t):
            if t < 2:
                return csrc.v((s, sl(t * 128, 128), ALL), slots=None if csrc is self.c_in else [s * 2 + t])
            tt_ = t - 2
            return xsrc.v((s, sl(tt_ * 128, 128), ALL), slots=None if xsrc is self.x_in else [s * 16 + tt_])

        vec = fw.sb([128, NV], F32)
        fw.dma("sp", vec[:], self.vecs.v((l, ALL, ALL)))
        negb = fw.sb([128, 4], F32)
        fw.ts("dve", negb[:], vec.v((ALL, sl(31, 4))), -1.0, ALU.mult)

        Acol = fw.sb([128, 8, 5], F32)
        Scol = fw.sb([128, 8, 5], F32)
        m0 = fw.mark()
        modw = fw.sb([128, 8, 2048], BF16)
        self.wload(modw[:], self.mod_w, self.mod_w.t[l, :, 0:2048].rearrange("(kc p) c -> p kc c", p=128))
        pm = self.ps()
        for ch in range(16):
            for kc in range(8):
                fw.mm(pm.v((ALL, sl(ch * 5, 5))), modw.v((ALL, kc, sl(ch * 128, 128))), self.scT.v((ALL, kc, ALL)),
                      start=(kc == 0), stop=(kc == 7))
        for ch in range(8):
            fw.ts("dve", Scol.v((ALL, ch, ALL)), pm.v((ALL, sl(ch * 5, 5))), vec.v((ALL, sl(8 + ch, 1))), ALU.add)
            fw.ts("dve", Acol.v((ALL, ch, ALL)), pm.v((ALL, sl((8 + ch) * 5, 5))), vec.v((ALL, sl(16 + ch, 1))), ALU.add,
                  1.0, ALU.add)
            fw.ts("dve", Acol.v((ALL, ch, ALL)), Acol.v((ALL, ch, ALL)), vec.v((ALL, sl(ch, 1))), ALU.mult)
        fw.barrier()
        fw.release(m0)

        hT = fw.sb([128, 8, T], BF16, nslots=5)
        ymla = fw.sb([128, 4, T], BF16, nslots=5)
        ypool = fw.sb([128, 4, T], BF16, nslots=5)
        ygla = fw.sb([128, 4, T], BF16, nslots=5)

        m0 = fw.mark()
        xt = [fw.sb([128, D], F32) for _ in range(3)]
        xn = [fw.sb([128, D], BF16) for _ in range(2)]
        junk = fw.sb([128, D], BF16)
        st = [fw.sb([128, 4], F32) for _ in range(2)]
        for t in range(NT):
            c = chunk_of_tile(t)
            j = 4 if t < 2 else s
            X = xt[t % 3]
            fw.dma("sp", X[:], xtile_src(t))
            S_ = st[t % 2]
            fw.act(junk[:], X[:], AF.Square, accum=S_.v((ALL, sl(0, 1))))
            fw.act(S_.v((ALL, sl(1, 1))), S_.v((ALL, sl(0, 1))), AF.Ln, scale=1.0 / D, bias=EPS)
            fw.act(S_.v((ALL, sl(2, 1))), S_.v((ALL, sl(1, 1))), AF.Exp, scale=-0.5)
            XN = xn[t % 2]
            fw.ts("dve", XN[:], X[:], S_.v((ALL, sl(2, 1))), ALU.mult)
            for kc in range(8):
                fw.tr(self.pst.v((ALL, sl(kc * 128, 128))), XN.v((ALL, sl(kc * 128, 128))), self.ident, inc=(kc == 7))
            for kc in range(8):
                o = hT.v((ALL, kc, sl(t * 128, 128)), slots=[c])
                i_ = self.pst.v((ALL, sl(kc * 128, 128)))
                if kc % 2 == 0:
                    fw.act(o, i_, AF.Identity, scale=Acol.v((ALL, kc, sl(j, 1))), bias=Scol.v((ALL, kc, sl(j, 1))))
                else:
                    fw.ts("dve", o, i_, Acol.v((ALL, kc, sl(j, 1))), ALU.mult, Scol.v((ALL, kc, sl(j, 1))), ALU.add)
        fw.barrier()
        fw.release(m0)

        import os as _os
        stop = _os.environ.get("KSTOP", "all")
        if stop == "p0":
            return
        self.mla(s, l, hT, ymla, vec, c_lo)
        if stop == "mla":
            return
        self.pool(s, l, hT, ypool, vec, c_lo)
        if stop == "pool":
            return
        self.gla(s, l, hT, ygla, vec, negb, c_lo, t_lo)
        if stop == "gla":
            return
        self.merge_out(s, l, hT, ymla, ypool, ygla, vec, c_lo, t_lo, xtile_src, last_layer)
        fw.barrier()
        fw.release(m_blk)

    def mla(self, s, l, hT, ymla, vec, c_lo):
        fw = self.fw
        m0 = fw.mark()
        cs = fw.sb([128, 2, 512], F32)

        def load_cs(c):
            a, n = TC[c]
            fw.dma("sp", cs.v((sl(64, 32), ALL, sl(0, n))), self.k_cs.v((sl(64, 32), ALL, sl(a, n))))
        kvn = fw.sb([128, T], BF16, nslots=5)
        krope = fw.sb([128, T], BF16, nslots=5)
        qn = fw.sb([128, 2, T], BF16, nslots=5)
        wukv = fw.sb([128, 1024], BF16)
        self.wload(wukv[:], self.w_ukv, self.w_ukv.t[l, :, :])
        wuq = fw.sb([128, 2, 768], BF16)
        self.wload(wuq[:], self.w_uq, self.w_uq.t[l, :, :].rearrange("(kc p) c -> p kc c", p=128))
        wuqr = fw.sb([128, 2, 768], BF16)
        fw.memset("pool", wuqr[:], 0.0)
        for h in range(8):
            for a in range(2):
                c0 = h * 96 + 64 + 16 * a
                fw.ts("dve", wuqr.v((ALL, ALL, sl(c0, 8))), wuq.v((ALL, ALL, sl(c0 + 8, 8))), -1.0, ALU.mult)
                fw.copy("dve", wuqr.v((ALL, ALL, sl(c0 + 8, 8))), wuq.v((ALL, ALL, sl(c0, 8))))
        sq = [fw.sb([128, 512], BF16) for _ in range(2)]
        lnb = fw.sb([128, 512], F32)
        rstd = fw.sb([128, 512], F32)
        t1 = fw.sb([128, 512], F32)
        t2 = fw.sb([128, 512], F32)

        m1 = fw.mark()
        wkv = fw.sb([128, 8, 160], BF16)
        self.win_block(wkv[:], l, C_KV, 160)
        wkr = fw.sb([128, 8, 96], BF16)
        fw.memset("pool", wkr[:], 0.0)
        for a in range(2):
            c0 = 128 + 16 * a
            d0 = 64 + 16 * a
            fw.ts("dve", wkr.v((ALL, ALL, sl(d0, 8))), wkv.v((ALL, ALL, sl(c0 + 8, 8))), -1.0, ALU.mult)
            fw.copy("dve", wkr.v((ALL, ALL, sl(d0 + 8, 8))), wkv.v((ALL, ALL, sl(c0, 8))))
        for c in range(5):
            a, n = TC[c]
            load_cs(c)
            pkv = self.ps()
            self.zT(pkv, wkv, c, hT, 0, 128)
            pkr = self.ps()
            self.zT(pkr, wkv, c, hT, 64, 96)
            pro = self.ps()
            self.zT(pro, wkr, c, hT, 0, 96)
            SQ = sq[c % 2]
            fw.act(SQ.v((ALL, sl(0, n))), pkv.v((ALL, sl(0, n))), AF.Square)
            pss = self.ps()
            fw.mm(pss.v((ALL, sl(0, n))), self.ones, SQ.v((ALL, sl(0, n))))
            self.rstd_bcast(pss, n, 128, lnb, rstd)
            fw.stt(kvn.v((ALL, sl(a, n)), slots=[c]), pkv.v((ALL, sl(0, n))), vec.v((ALL, sl(26, 1))),
                   rstd.v((ALL, sl(0, n))), ALU.mult, ALU.mult)
            R = sl(64, 32)
            fw.tt("dve", t1.v((R, sl(0, n))), pkr.v((R, sl(0, n))), cs.v((R, 0, sl(0, n))), ALU.mult)
            fw.tt("dve", t2.v((R, sl(0, n))), pro.v((R, sl(0, n))), cs.v((R, 1, sl(0, n))), ALU.mult)
            fw.tt("pool", krope.v((R, sl(a, n)), slots=[c]), t1.v((R, sl(0, n))), t2.v((R, sl(0, n))), ALU.add)
        fw.barrier()
        fw.release(m1)

        m1 = fw.mark()
        wq = fw.sb([128, 8, 256], BF16)
        self.win_block(wq[:], l, C_Q, 256)
        for c in range(c_lo, 5):
            a, n = TC[c]
            pq = [self.ps(), self.ps()]
            pss = self.ps()
            for k2 in range(2):
                self.zT(pq[k2], wq, c, hT, k2 * 128, 128)
                fw.act(sq[k2].v((ALL, sl(0, n))), pq[k2].v((ALL, sl(0, n))), AF.Square)
            for k2 in range(2):
                fw.mm(pss.v((ALL, sl(0, n))), self.ones, sq[k2].v((ALL, sl(0, n))), start=(k2 == 0), stop=(k2 == 1))
            self.rstd_bcast(pss, n, 256, lnb, rstd)
            for k2 in range(2):
                fw.stt(qn.v((ALL, k2, sl(a, n)), slots=[c]), pq[k2].v((ALL, sl(0, n))), vec.v((ALL, sl(24 + k2, 1))),
                       rstd.v((ALL, sl(0, n))), ALU.mult, ALU.mult)
        fw.barrier()
        fw.release(m1)

        KT = [fw.sb([128, T], BF16, nslots=5) for _ in range(2)]
        QT = [fw.sb([128, T], BF16, nslots=5) for _ in range(2)]
        VP = [fw.sb([128, NT, 128], BF16, nslots=NT) for _ in range(2)]
        for i in range(2):
            fw.memset("pool", VP[i][:], 0.0)
        sg = fw.sb([128, T], BF16, nslots=5)
        wg = [fw.sb([128, 8, 128], BF16) for _ in range(2)]
        PT = [fw.sb([128, 512], BF16) for _ in range(4)]
        rec = lnb
        accO, accD = self.psb[5], self.psb[6]
        npt = 0
        for hp in range(4):
            WG = wg[hp % 2]
            self.win_block(WG[:], l, C_MG + hp * 128, 128)
            for c in range(c_lo, 5):
                a, n = TC[c]
                pg = self.ps()
                self.zT(pg, WG, c, hT)
                fw.act(sg.v((ALL, sl(a, n)), slots=[c]), pg.v((ALL, sl(0, n))), AF.Silu)
            for i in range(2):
                h = hp * 2 + i
                for c in range(5):
                    a, n = TC[c]
                    pk = self.ps()
                    fw.mm(pk.v((sl(0, 64), sl(0, n))), wukv.v((ALL, sl(h * 128, 64))), kvn.v((ALL, sl(a, n)), slots=[c]))
                    fw.copy("act", KT[i].v((sl(0, 64), sl(a, n)), slots=[c]), pk.v((sl(0, 64), sl(0, n))))
                    fw.copy("pool", KT[i].v((sl(64, 32), sl(a, n)), slots=[c]), krope.v((sl(64, 32), sl(a, n)), slots=[c]))
                for c in range(c_lo, 5):
                    a, n = TC[c]
                    load_cs(c)
                    p1, p2 = self.ps(), self.ps()
                    for k2 in range(2):
                        fw.mm(p1.v((sl(0, 96), sl(0, n))), wuq.v((ALL, k2, sl(h * 96, 96))),
                              qn.v((ALL, k2, sl(a, n)), slots=[c]), start=(k2 == 0), stop=(k2 == 1))
                    for k2 in range(2):
                        fw.mm(p2.v((sl(0, 96), sl(0, n))), wuqr.v((ALL, k2, sl(h * 96, 96))),
                              qn.v((ALL, k2, sl(a, n)), slots=[c]), start=(k2 == 0), stop=(k2 == 1))
                    fw.copy("act", QT[i].v((sl(0, 64), sl(a, n)), slots=[c]), p1.v((sl(0, 64), sl(0, n))))
                    R = sl(64, 32)
                    fw.tt("dve", t1.v((R, sl(0, n))), p1.v((R, sl(0, n))), cs.v((R, 0, sl(0, n))), ALU.mult)
                    fw.tt("dve", t2.v((R, sl(0, n))), p2.v((R, sl(0, n))), cs.v((R, 1, sl(0, n))), ALU.mult)
                    fw.tt("pool", QT[i].v((R, sl(a, n)), slots=[c]), t1.v((R, sl(0, n))), t2.v((R, sl(0, n))), ALU.add)
            for t in range(NT):
                pv = self.ps()
                vsrc = wukv.ap(wukv.t[:, hp * 256:hp * 256 + 256].rearrange("p (h e) -> p h e", h=2)[:, :, 64:128])
                fw.mm(pv.v((ALL, sl(0, 128))), kvn.v((ALL, sl(t * 128, 128)), slots=[chunk_of_tile(t)]), vsrc)
                fw.copy("act", VP[0].v((ALL, t, sl(0, 64)), slots=[t]), pv.v((ALL, sl(0, 64))))
                fw.copy("dve", VP[1].v((ALL, t, sl(64, 64)), slots=[t]), pv.v((ALL, sl(64, 64))))
            for c in range(c_lo, 5):
                a, n = TC[c]
                kts = range(0, 2) if c == 0 else range(0, NT)
                nk = len(kts)
                def emit_S(kt, npt0):
                    pts = []
                    for i in range(2):
                        pS = self.ps()
                        fw.mm(pS.v((ALL, sl(0, n))), KT[i].v((sl(0, 96), sl(kt * 128, 128)), slots=[chunk_of_tile(kt)]),
                              QT[i].v((sl(0, 96), sl(a, n)), slots=[c]))
                        P = PT[(npt0 + i) % 4]
                        fw.act(P.v((ALL, sl(0, n))), pS.v((ALL, sl(0, n))), AF.Exp, scale=ATT_SCALE)
                        pts.append(P)
                    return pts

                kl = list(kts)
                cur_pts = emit_S(kl[0], npt)
                npt += 2
                for ki, kt in enumerate(kl):
                    nxt_pts = None
                    if ki + 1 < nk:
                        nxt_pts = emit_S(kl[ki + 1], npt)
                        npt += 2
                    pts = cur_pts
                    fw.mm(accO.v((ALL, sl(0, n))), VP[0].v((ALL, kt, ALL), slots=[kt]), pts[0].v((ALL, sl(0, n))),
                          start=(ki == 0), stop=False, inc=False)
                    fw.mm(accO.v((ALL, sl(0, n))), VP[1].v((ALL, kt, ALL), slots=[kt]), pts[1].v((ALL, sl(0, n))),
                          start=False, stop=(ki == nk - 1), inc=False)
                    fw.mm(accD.v((ALL, sl(0, n))), self.onesE, pts[0].v((ALL, sl(0, n))),
                          start=(ki == 0), stop=False, inc=False)
                    fw.mm(accD.v((ALL, sl(0, n))), self.onesO, pts[1].v((ALL, sl(0, n))),
                          start=False, stop=(ki == nk - 1), inc=True)
                    cur_pts = nxt_pts
                fw.act(rec.v((ALL, sl(0, n))), accD.v((ALL, sl(0, n))), AF.Ln)
                fw.act(rec.v((ALL, sl(0, n))), rec.v((ALL, sl(0, n))), AF.Exp, scale=-1.0)
                fw.tt("dve", t1.v((ALL, sl(0, n))), accO.v((ALL, sl(0, n))), rec.v((ALL, sl(0, n))), ALU.mult)
                fw.tt("pool", ymla.v((ALL, hp, sl(a, n)), slots=[c]), t1.v((ALL, sl(0, n))),
                      sg.v((ALL, sl(a, n)), slots=[c]), ALU.mult)
        fw.barrier()
        fw.release(m0)

    def pool(self, s, l, hT, ypool, vec, c_lo):
        fw = self.fw
        m0 = fw.mark()
        U = fw.sb([128, UL], F32)
        A = fw.sb([128, UL], F32)
        B = fw.sb([128, UL], F32)
        PP = fw.sb([128, UL], BF16)
        invc = fw.sb([128, UL], F32)
        sgp = [fw.sb([128, 512], BF16) for _ in range(2)]
        wu = [fw.sb([128, 8, 128], BF16) for _ in range(2)]
        wg = [fw.sb([128, 8, 128], BF16) for _ in range(2)]
        pw = [fw.sb([128, 128], BF16) for _ in range(2)]
        fw.memset("pool", U[:], 0.0)
        for g, w in enumerate((2, 4, 8, 16)):
            WU, WG, PW = wu[g % 2], wg[g % 2], pw[g % 2]
            self.win_block(WU[:], l, C_PX + g * 128, 128)
            self.win_block(WG[:], l, C_PG + g * 128, 128)
            self.wload(PW[:], self.pool_w, self.pool_w.t[l, g, :, :])
            fw.dma("sp", invc[:], self.k_invc.v((ALL, g, ALL)))
            for c in range(5):
                a, n = TC[c]
                pu = self.ps()
                self.zT(pu, WU, c, hT)
                fw.copy("act", U.v((ALL, sl(upos(a), n))), pu.v((ALL, sl(0, n))))
            src, width = U, 1
            bufs = [A, B]
            bi = 0
            while width < w:
                dst = bufs[bi]
                bi ^= 1
                L = UL - 2 * width + 1
                fw.tt("pool" if width > 1 else "dve", dst.v((ALL, sl(0, L))), src.v((ALL, sl(0, L))),
                      src.v((ALL, sl(width, L))), ALU.add)
                src = dst
                width *= 2
            hw_ = w // 2
            L = UL - 16
            tmp = bufs[bi]
            fw.tt("dve", tmp.v((ALL, sl(8, L))), src.v((ALL, sl(8 - hw_, L))), invc.v((ALL, sl(8, L))), ALU.mult)
            fw.tt("pool", PP.v((ALL, sl(8, L))), tmp.v((ALL, sl(8, L))), U.v((ALL, sl(8, L))), ALU.subtract)
            for c in range(c_lo, 5):
                a, n = TC[c]
                pm = self.ps()
                fw.mm(pm.v((ALL, sl(0, n))), PW[:], PP.v((ALL, sl(upos(a), n))))
                pg = self.ps()
                self.zT(pg, WG, c, hT)
                SG = sgp[c % 2]
                fw.act(SG.v((ALL, sl(0, n))), pg.v((ALL, sl(0, n))), AF.Silu)
                fw.stt(ypool.v((ALL, g, sl(a, n)), slots=[c]), pm.v((ALL, sl(0, n))), vec.v((ALL, sl(27 + g, 1))),
                       SG.v((ALL, sl(0, n))), ALU.mult, ALU.mult)
        fw.barrier()
        fw.release(m0)

    def gla(self, s, l, hT, ygla, vec, negb, c_lo, t_lo):
        fw = self.fw
        m0 = fw.mark()
        obuf = fw.sb([128, 4, T], BF16, nslots=NT)
        m_q = fw.mark()
        elast = fw.sb([128, 2, 2, NT], F32)
        wab = fw.sb([128, 8, 32], BF16)
        self.win_block(wab[:], l, C_AF, 32)
        zab = fw.sb([32, T], BF16, nslots=5)
        wpad = fw.sb([32, 2, 256], BF16)
        fw.memset("pool", wpad[:], 0.0)
        self.wload(wpad.v((sl(0, 16), 0, ALL)), self.af_w2, self.af_w2.t[l, :, :])
        self.wload(wpad.v((sl(16, 16), 1, ALL)), self.ab_w2, self.ab_w2.t[l, :, :])
        for c in range(5):
            a, n = TC[c]
            pz = self.ps()
            self.zT(pz, wab, c, hT, 0, 32)
            fw.copy("act", zab.v((ALL, sl(a, n)), slots=[c]), pz.v((sl(0, 32), sl(0, n))))
        vtm = fw.sb([128, NT, 512], BF16, nslots=NT)
        m1 = fw.mark()
        wv = [fw.sb([128, 8, 256], BF16) for _ in range(2)]
        for i in range(2):
            self.win_block(wv[i][:], l, C_GV + 256 * i, 256)
        for t in range(NT):
            c = chunk_of_tile(t)
            pv = self.ps()
            for i in range(2):
                for kc in range(8):
                    fw.mm(pv.v((ALL, sl(i * 256, 256))), hT.v((ALL, kc, sl(t * 128, 128)), slots=[c]), wv[i].v((ALL, kc, ALL)),
                          start=(kc == 0), stop=(kc == 7))
            fw.copy("dve", vtm.v((ALL, t, ALL), slots=[t]), pv[:])
        fw.barrier()
        fw.release(m1)
        qx = [fw.sb([128, T], BF16, nslots=5) for _ in range(2)]
        kx = [fw.sb([128, T], BF16, nslots=5) for _ in range(2)]
        nat = 0
        for dr in (1, 0):
            m1 = fw.mark()
            sp = fw.sb([128, 512], F32)
            cs = fw.sb([128, 512], F32)
            c2 = fw.sb([128, 512], F32)
            eQ = fw.sb([128, 512], F32)
            eK = fw.sb([128, 512], F32)
            et = fw.sb([128, 512], F32)
            wq = [fw.sb([128, 8, 128], BF16) for _ in range(2)]
            wk = [fw.sb([128, 8, 128], BF16) for _ in range(2)]
            for fc in range(2):
                self.win_block(wq[fc][:], l, C_GQ + fc * 128, 128)
                self.win_block(wk[fc][:], l, C_GK + fc * 128, 128)
            for fc in range(2):
                for c in range(5):
                    a, n = TC[c]
                    nt_ = n // 128
                    px = self.ps()
                    fw.mm(px.v((ALL, sl(0, n))), wpad.v((ALL, dr, sl(fc * 128, 128))), zab.v((ALL, sl(a, n)), slots=[c]))
                    fw.act(et.v((ALL, sl(0, n))), px.v((ALL, sl(0, n))), AF.Exp, scale=-1.0,
                           bias=negb.v((ALL, sl(dr * 2 + fc, 1))))
                    fw.act(sp.v((ALL, sl(0, n))), et.v((ALL, sl(0, n))), AF.Ln, bias=1.0)
                    bufs3 = [sp, cs, c2]
                    cur = 0
                    k = 1
                    while k < 128:
                        nxt = 1 if cur != 1 else 2
                        vi = bufs3[cur].t[:, 0:n].rearrange("p (t i) -> p t i", i=128)
                        vo = bufs3[nxt].t[:, 0:n].rearrange("p (t i) -> p t i", i=128)
                        I, O = bufs3[cur], bufs3[nxt]
                        if dr == 0:
                            fw.tt("dve", O.ap(vo[:, :, k:128]), I.ap(vi[:, :, k:128]), I.ap(vi[:, :, 0:128 - k]), ALU.add)
                            fw.copy("pool", O.ap(vo[:, :, 0:k]), I.ap(vi[:, :, 0:k]))
                        else:
                            fw.tt("dve", O.ap(vo[:, :, 0:128 - k]), I.ap(vi[:, :, 0:128 - k]), I.ap(vi[:, :, k:128]), ALU.add)
                            fw.copy("pool", O.ap(vo[:, :, 128 - k:128]), I.ap(vi[:, :, 128 - k:128]))
                        cur = nxt
                        k *= 2
                    assert cur == 1
                    fw.act(eQ.v((ALL, sl(0, n))), cs.v((ALL, sl(0, n))), AF.Exp, scale=-1.0 / 16)
                    fw.act(eK.v((ALL, sl(0, n))), cs.v((ALL, sl(0, n))), AF.Exp, scale=1.0 / 16)
                    pos = 127 if dr == 0 else 0
                    fw.copy("dve", elast.v((ALL, dr, fc, sl(a // 128, nt_))),
                            eQ.ap(eQ.t[:, 0:n].rearrange("p (t i) -> p t i", i=128)[:, :, pos]))
                    pq = self.ps()
                    self.zT(pq, wq[fc], c, hT)
                    fw.stt(qx[fc].v((ALL, sl(a, n)), slots=[c]), pq.v((ALL, sl(0, n))), 0.125,
                           eQ.v((ALL, sl(0, n))), ALU.mult, ALU.mult)
                    pk = self.ps()
                    self.zT(pk, wk[fc], c, hT)
                    fw.tt("dve", kx[fc].v((ALL, sl(a, n)), slots=[c]), pk.v((ALL, sl(0, n))),
                          eK.v((ALL, sl(0, n))), ALU.mult)
            fw.barrier()
            fw.release(m1)
            m1 = fw.mark()
            ktm = fw.sb([128, NT, 256], BF16, nslots=NT)
            S32 = fw.sb([128, 2, 128], F32, nslots=4)
            Sbf = fw.sb([128, 2, 128], BF16, nslots=4)
            stmp = fw.sb([128, 2, 128], F32, nslots=4)
            atm = [fw.sb([128, 128], BF16) for _ in range(4)]
            osum = [fw.sb([128, 128], F32) for _ in range(2)]
            for t in range(NT):
                c = chunk_of_tile(t)
                for fc in range(2):
                    fw.tr(self.pst.v((ALL, sl(fc * 128, 128))), kx[fc].v((ALL, sl(t * 128, 128)), slots=[c]),
                          self.ident, inc=(fc == 1))
                fw.copy("act", ktm.v((ALL, t, ALL), slots=[t]), self.pst.v((ALL, sl(0, 256))))
            order = ([1, 0] + list(range(NT - 1, 1, -1))) if dr == 1 else list(range(NT))
            mask = self.maskb if dr == 1 else self.maskf
            for oi, t in enumerate(order):
                c = chunk_of_tile(t)
                need_out = t >= t_lo
                for h in range(4):
                    fc, r0 = h // 2, (h % 2) * 64
                    RR = sl(r0, 64)
                    qv = qx[fc].v((RR, sl(t * 128, 128)), slots=[c])
                    kv_ = kx[fc].v((RR, sl(t * 128, 128)), slots=[c])
                    vv = vtm.v((ALL, t, sl(h * 128, 128)), slots=[t])
                    if need_out:
                        pa = self.ps()
                        fw.mm(pa.v((ALL, sl(0, 128))), kv_, qv)
                        AT = atm[nat % 4]
                        nat += 1
                        fw.tt("dve", AT[:], pa.v((ALL, sl(0, 128))), mask, ALU.mult)
                        po = self.ps()
                        fw.mm(po.v((ALL, sl(0, 128))), vv, AT[:], start=True, stop=(oi == 0))
                        if oi > 0:
                            fw.mm(po.v((ALL, sl(0, 128))), Sbf.v((RR, fc, ALL), slots=[h]), qv, start=False, stop=True)
                        ov = obuf.v((ALL, h, sl(t * 128, 128)), slots=[t])
                        if dr == 1:
                            fw.copy("act", ov, po.v((ALL, sl(0, 128))))
                        else:
                            fw.tt("dve", ov, po.v((ALL, sl(0, 128))), ov, ALU.add)
                    if oi == len(order) - 1:
                        continue
                    psu = self.ps()
                    fw.mm(psu.v((ALL, sl(0, 128))), ktm.v((ALL, t, sl(fc * 128, 128)), slots=[t]), vv)
                    ecol = elast.v((RR, dr, fc, sl(t, 1)))
                    sv = S32.v((RR, fc, ALL), slots=[h])
                    if oi == 0:
                        fw.ts("dve", sv, psu.v((RR, sl(0, 128))), ecol, ALU.mult)
                    else:
                        tv = stmp.v((RR, fc, ALL), slots=[h])
                        fw.ts("pool", tv, sv, ecol, ALU.mult)
                        fw.stt(sv, psu.v((RR, sl(0, 128))), ecol, tv, ALU.mult, ALU.add)
                    fw.copy("act", Sbf.v((RR, fc, ALL), slots=[h]), sv)
            fw.barrier()
            fw.release(m1)
        fw.barrier()
        fw.release(m_q)

        sq = [fw.sb([128, 512], BF16) for _ in range(2)]
        lnb = fw.sb([128, 512], F32)
        rstd = fw.sb([128, 512], F32)
        t1 = fw.sb([128, 512], F32)
        sgp = [fw.sb([128, 512], BF16) for _ in range(2)]
        wg = [fw.sb([128, 8, 128], BF16) for _ in range(2)]
        for h in range(4):
            WG = wg[h % 2]
            self.win_block(WG[:], l, C_GG + h * 128, 128)
            for c in range(c_lo, 5):
                a, n = TC[c]
                tl = [t for t in range(NT) if chunk_of_tile(t) == c]
                ov = obuf.v((ALL, h, sl(a, n)), slots=tl)
                SQ = sq[c % 2]
                fw.act(SQ.v((ALL, sl(0, n))), ov, AF.Square)
                pss = self.ps()
                fw.mm(pss.v((ALL, sl(0, n))), self.ones, SQ.v((ALL, sl(0, n))))
                self.rstd_bcast(pss, n, 128, lnb, rstd)
                pg = self.ps()
                self.zT(pg, WG, c, hT)
                SG = sgp[c % 2]
                fw.act(SG.v((ALL, sl(0, n))), pg.v((ALL, sl(0, n))), AF.Silu)
                fw.stt(t1.v((ALL, sl(0, n))), ov, vec.v((ALL, sl(35, 1))), rstd.v((ALL, sl(0, n))), ALU.mult, ALU.mult)
                fw.tt("pool", ygla.v((ALL, h, sl(a, n)), slots=[c]), t1.v((ALL, sl(0, n))), SG.v((ALL, sl(0, n))), ALU.mult)
        fw.barrier()
        fw.release(m0)

    def merge_out(self, s, l, hT, ymla, ypool, ygla, vec, c_lo, t_lo, xtile_src, last_layer):
        fw = self.fw
        m0 = fw.mark()
        merged = fw.sb([128, 8, T], BF16, nslots=5)
        m1 = fw.mark()
        ys = [ymla, ypool, ygla]
        wb = [fw.sb([128, 4, D], BF16) for _ in range(3)]
        for b in range(3):
            self.wload(wb[b][:], self.w_b[b], self.w_b[b].t[l, :, :].rearrange("(kc p) c -> p kc c", p=128))
        wg = [fw.sb([128, 8, 3, 128], BF16) for _ in range(2)]
        gs = [fw.sb([128, 512], BF16) for _ in range(3)]
        ta = fw.sb([128, 512], F32)
        tb = fw.sb([128, 512], F32)
        for d in range(8):
            WG = wg[d % 2]
            for b in range(3):
                src = self.w_in.t[l, :, C_MRG + b * D + d * 128:C_MRG + b * D + d * 128 + 128].rearrange(
                    "(kc p) c -> p kc c", p=128)
                self.wload(WG.v((ALL, ALL, b, ALL)), self.w_in, src)
            for c in range(c_lo, 5):
                a, n = TC[c]
                pj = []
                for b in range(3):
                    pg = self.ps()
                    for kc in range(8):
                        fw.mm(pg.v((ALL, sl(0, n))), WG.v((ALL, kc, b, ALL)), hT.v((ALL, kc, sl(a, n)), slots=[c]),
                              start=(kc == 0), stop=(kc == 7))
                    fw.act(gs[b].v((ALL, sl(0, n))), pg.v((ALL, sl(0, n))), AF.Sigmoid)
                for b in range(3):
                    pp = self.ps() if b < 2 else self.psb[5]
                    for kc in range(4):
                        fw.mm(pp.v((ALL, sl(0, n))), wb[b].v((ALL, kc, sl(d * 128, 128))),
                              ys[b].v((ALL, kc, sl(a, n)), slots=[c]), start=(kc == 0), stop=(kc == 3))
                    pj.append(pp)
                fw.tt("dve", ta.v((ALL, sl(0, n))), pj[0].v((ALL, sl(0, n))), gs[0].v((ALL, sl(0, n))), ALU.mult)
                fw.tt("dve", tb.v((ALL, sl(0, n))), pj[1].v((ALL, sl(0, n))), gs[1].v((ALL, sl(0, n))), ALU.mult)
                fw.tt("pool", ta.v((ALL, sl(0, n))), ta.v((ALL, sl(0, n))), tb.v((ALL, sl(0, n))), ALU.add)
                fw.tt("dve", tb.v((ALL, sl(0, n))), pj[2].v((ALL, sl(0, n))), gs[2].v((ALL, sl(0, n))), ALU.mult)
                fw.tt("pool", merged.v((ALL, d, sl(a, n)), slots=[c]), ta.v((ALL, sl(0, n))), tb.v((ALL, sl(0, n))), ALU.add)
        fw.barrier()
        fw.release(m1)

        GP = {}
        jl = [4, s] if t_lo == 0 else [s]
        for j in jl:
            GP[j] = fw.sb([128, D], F32)
        m2 = fw.mark()
        rows = fw.sb([128, 2 * D], F32)
        fw.dma("sp", rows[:], self.rows.v((l, ALL, ALL)))
        modg = fw.sb([128, 8, D], BF16)
        self.wload(modg[:], self.mod_w, self.mod_w.t[l, :, 2048:3072].rearrange("(kc p) c -> p kc c", p=128))
        rep = fw.sb([128, 8, 128], BF16)
        for j in jl:
            G = GP[j]
            for kc in range(8):
                fw.ts("dve", rep.v((ALL, kc, ALL)), self.onesf[:], self.scT.v((ALL, kc, sl(j, 1))), ALU.mult)
            for hf in range(2):
                pg = self.ps()
                for kc in range(8):
                    fw.mm(pg[:], rep.v((ALL, kc, ALL)), modg.v((ALL, kc, sl(hf * 512, 512))), start=(kc == 0), stop=(kc == 7))
                fw.tt("dve", G.v((ALL, sl(hf * 512, 512))), pg[:], rows.v((ALL, sl(D + hf * 512, 512))), ALU.add)
                fw.tt("pool", G.v((ALL, sl(hf * 512, 512))), G.v((ALL, sl(hf * 512, 512))), rows.v((ALL, sl(hf * 512, 512))), ALU.mult)
        fw.barrier()
        fw.release(m2)
        wout = fw.sb([128, 8, D], BF16)
        self.wload(wout[:], self.w_out, self.w_out.t[l, :, :].rearrange("(kc p) c -> p kc c", p=128))
        xt = [fw.sb([128, D], F32) for _ in range(2)]
        tm = [fw.sb([128, D], F32) for _ in range(2)]
        xo = tm
        junk = fw.sb([128, 512], BF16)
        st = [fw.sb([128, 8], F32) for _ in range(2)]
        for t in range(t_lo, NT):
            c = chunk_of_tile(t)
            j = 4 if t < 2 else s
            X = xt[t % 2]
            fw.dma("sp", X[:], xtile_src(t))
            po = [self.ps(), self.ps()]
            S_ = st[t % 2]
            for hf in range(2):
                for kc in range(8):
                    fw.mm(po[hf][:], merged.v((ALL, kc, sl(t * 128, 128)), slots=[c]), wout.v((ALL, kc, sl(hf * 512, 512))),
                          start=(kc == 0), stop=(kc == 7))
                fw.act(junk[:], po[hf][:], AF.Square, accum=S_.v((ALL, sl(hf, 1))))
            fw.tt("dve", S_.v((ALL, sl(2, 1))), S_.v((ALL, sl(0, 1))), S_.v((ALL, sl(1, 1))), ALU.add)
            fw.act(S_.v((ALL, sl(3, 1))), S_.v((ALL, sl(2, 1))), AF.Ln, scale=1.0 / D, bias=EPS)
            fw.act(S_.v((ALL, sl(4, 1))), S_.v((ALL, sl(3, 1))), AF.Exp, scale=-0.5)
            TM, XO = tm[t % 2], xo[t % 2]
            for hf in range(2):
                H = sl(hf * 512, 512)
                fw.stt(TM.v((ALL, H)), po[hf][:], S_.v((ALL, sl(4, 1))), GP[j].v((ALL, H)), ALU.mult, ALU.mult)
            fw.tt("pool", XO[:], TM[:], X[:], ALU.add)
            if t < 2:
                dst = self.c_out.v((s, sl(t * 128, 128), ALL), slots=[s * 2 + t])
            else:
                dst = self.x_out.v((s, sl((t - 2) * 128, 128), ALL), slots=[s * 16 + t - 2])
            mk = fw.dma("sp", dst, XO[:])
            if last_layer or (t < 2 and self.layers == [0]) or self.layers == [0]:
                self.out_marks.append(mk)
        fw.barrier()
        fw.release(m0)


_CACHE = {}


def _prog(nseq, layers):
    key = (nseq, tuple(layers))
    if key not in _CACHE:
        _CACHE[key] = Prog(nseq, list(layers)).nc
    return _CACHE[key]


FUSED = False
LAYER_FUSED = True


def kernel(**inp):
    inp = {k: np.asarray(v) for k, v in inp.items()}
    ncore, nseq = 8, 4
    cm, cs, invc, sm = host_consts()
    x = np.ascontiguousarray(inp["x"], np.float32)
    ctx = np.ascontiguousarray(inp["ctx"], np.float32)
    shared = {k: np.ascontiguousarray(inp[k], np.float32) for k in (
        "w_in", "mod_w", "mla_w_uq", "mla_w_ukv", "pool_w", "gla_af_w2", "gla_ab_w2",
        "w_branch_mla", "w_branch_pool", "w_branch_gla", "w_out")}
    shared["vecs"] = np.stack([host_vecs(inp, l) for l in range(DEPTH)])
    shared["rows"] = np.stack([np.concatenate([
        np.broadcast_to(inp["post_norm"][l][None, :], (128, D)),
        np.broadcast_to(inp["mod_b"][l][None, 2048:3072], (128, D))], 1) for l in range(DEPTH)]).astype(np.float32)
    shared.update(k_cm=cm, k_cs=cs, k_invc=invc, k_sm=sm)
    def cT_of(rows5):
        return np.ascontiguousarray(rows5.T.reshape(8, 128, 5).transpose(1, 0, 2).reshape(128, 40), np.float32)

    xs = [np.ascontiguousarray(x[k * nseq:(k + 1) * nseq]) for k in range(ncore)]
    cx = [np.ascontiguousarray(ctx[k * nseq:(k + 1) * nseq]) for k in range(ncore)]
    zero3 = np.zeros((3, D), np.float32)
    if FUSED:
        nc = _prog(nseq, [0, 1])
        in_maps = []
        for k in range(ncore):
            cc = np.concatenate([inp["c"][k * nseq:(k + 1) * nseq], inp["c_ctx"][None, :]], 0)
            in_maps.append(dict(shared, x=xs[k], ctx=cx[k], cT=cT_of(cc)))
        res = run_bass_kernel_spmd(nc, in_maps, core_ids=list(range(ncore)))
        xs = [np.asarray(res.results[k]["xo"]) for k in range(ncore)]
        return np.concatenate(xs, 0).astype(np.float32)
    if LAYER_FUSED:
        nc = _prog(1, [0, 1])
        outs = [np.empty_like(a) for a in xs]
        for j in range(nseq):
            in_maps = []
            for k in range(ncore):
                cc = np.concatenate([inp["c"][k * nseq + j][None, :], zero3, inp["c_ctx"][None, :]], 0)
                in_maps.append(dict(shared, x=xs[k][j:j + 1], ctx=cx[k][j:j + 1], cT=cT_of(cc)))
            res = run_bass_kernel_spmd(nc, in_maps, core_ids=list(range(ncore)))
            for k in range(ncore):
                outs[k][j] = np.asarray(res.results[k]["xo"])[0]
        return np.concatenate(outs, 0).astype(np.float32)
    for l in range(DEPTH):
        nc = _prog(1, [l])
        nxs = [np.empty_like(a) for a in xs]
        ncx = [np.empty_like(a) for a in cx]
        for j in range(nseq):
            in_maps = []
            for k in range(ncore):
                cc = np.concatenate([inp["c"][k * nseq + j][None, :], zero3, inp["c_ctx"][None, :]], 0)
                in_maps.append(dict(shared, x=xs[k][j:j + 1], ctx=cx[k][j:j + 1], cT=cT_of(cc)))
            res = run_bass_kernel_spmd(nc, in_maps, core_ids=list(range(ncore)))
            for k in range(ncore):
                nxs[k][j] = np.asarray(res.results[k]["xo"])[0]
                if l == 0:
                    ncx[k][j] = np.asarray(res.results[k]["co"])[0]
        xs = nxs
        if l == 0:
            cx = ncx
    return np.concatenate(xs, 0).astype(np.float32)
```
